# Optimizing a Trainium2 kernel written in Bass

```python
import jax, jax.numpy as jnp
from jax import lax
import numpy as np

D_MODEL = 1024
BATCH = 2
SEQ = 8192
DEPTH = 1

HGRN_HEADS = 8
HGRN_DK = 128
HGRN_DV = D_MODEL // HGRN_HEADS
HGRN_K_TOTAL = HGRN_HEADS * HGRN_DK
HGRN_V_TOTAL = HGRN_HEADS * HGRN_DV
CHUNK = 64
CONV_CH = D_MODEL
CONV_K = 31
D_FF = 2816
FFN_RES = 0.5
EPS = 1e-6
IN_SPLITS = (
    HGRN_K_TOTAL,
    HGRN_K_TOTAL,
    HGRN_V_TOTAL,
    HGRN_V_TOTAL,
    CONV_CH,
    CONV_CH,
    D_MODEL,
    D_MODEL,
)
IN_COLS = sum(IN_SPLITS)

kernel_name = "hybrid_hgrn2_conformer_macaron"


def rms_norm(x, g):
    xf = x.astype(jnp.float32)
    xf = xf * lax.rsqrt(jnp.mean(xf * xf, axis=-1, keepdims=True) + EPS)
    return (xf * g.astype(jnp.float32)).astype(x.dtype)


def layer_norm(x, g, b):
    xf = x.astype(jnp.float32)
    mu = jnp.mean(xf, axis=-1, keepdims=True)
    var = jnp.mean(jnp.square(xf - mu), axis=-1, keepdims=True)
    y = (xf - mu) * lax.rsqrt(var + EPS) * g.astype(jnp.float32) + b.astype(jnp.float32)
    return y.astype(x.dtype)


def swiglu_ffn(h, w_gate, w_up, w_down):
    return (jax.nn.silu(h @ w_gate) * (h @ w_up)) @ w_down


def hgrn2_chunkwise(q, logf, k, v):
    B, L, H, dk = q.shape
    dv = v.shape[-1]
    nc = L // CHUNK

    def to_chunks(t):
        return t.reshape(B, nc, CHUNK, H, t.shape[-1]).transpose(1, 0, 3, 2, 4)

    causal = jnp.tril(jnp.ones((CHUNK, CHUNK), dtype=bool))[None, None, :, :, None]

    def step(S, inp):
        qc, lc, kc, vc = inp
        b = jnp.cumsum(lc, axis=2)
        diff = b[:, :, :, None, :] - b[:, :, None, :, :]
        decay = jnp.exp(jnp.where(causal, diff, -jnp.inf))
        scores = jnp.einsum('bhtk,bhtsk,bhsk->bhts', qc, decay, kc)
        o = jnp.einsum('bhts,bhsv->bhtv', scores, vc) \
            + jnp.einsum('bhtk,bhkv->bhtv', qc * jnp.exp(b), S)
        b_last = b[:, :, -1:, :]
        S = jnp.exp(b_last[:, :, 0, :])[..., None] * S \
            + jnp.einsum('bhsk,bhsv->bhkv', kc * jnp.exp(b_last - b), vc)
        return S, o

    S0 = jnp.zeros((B, H, dk, dv), jnp.float32)
    _, o = lax.scan(step, S0, (to_chunks(q), to_chunks(logf), to_chunks(k), to_chunks(v)))
    return o.transpose(1, 0, 3, 2, 4).reshape(B, L, H, dv)


def causal_depthwise_conv(u, w, b):
    y = lax.conv_general_dilated(
        u, w.astype(u.dtype)[:, None, :], window_strides=(1,), padding=[(CONV_K - 1, 0)],
        dimension_numbers=('NWC', 'WIO', 'NWC'), feature_group_count=u.shape[-1])
    return y + b.astype(u.dtype)


def setup_inputs(seed: int = 0) -> dict:
    key = jax.random.key(seed)
    ks = jax.random.split(key, 24)
    D, F, L = D_MODEL, D_FF, DEPTH

    def w(k, shape, fan_in):
        return jax.random.normal(k, shape, jnp.float32) * (fan_in ** -0.5)

    def gain(k, shape):
        return 1.0 + 0.05 * jax.random.normal(k, shape, jnp.float32)

    def bias(k, shape):
        return 0.02 * jax.random.normal(k, shape, jnp.float32)

    return {
        "x": jax.random.normal(ks[0], (BATCH, SEQ, D), jnp.float32),
        "ffn1_norm": gain(ks[1], (L, D)),
        "ffn1_w_gate": w(ks[2], (L, D, F), D),
        "ffn1_w_up": w(ks[3], (L, D, F), D),
        "ffn1_w_down": w(ks[4], (L, F, D), F),
        "mix_norm": gain(ks[5], (L, D)),
        "w_in": w(ks[6], (L, D, IN_COLS), D),
        "hgrn_lb_logits": 1.0 + 0.1 * jax.random.normal(ks[7], (L + 1, HGRN_K_TOTAL), jnp.float32),
        "hgrn_head_norm": gain(ks[8], (L, HGRN_DV)),
        "hgrn_w_o": w(ks[9], (L, HGRN_V_TOTAL, D), HGRN_V_TOTAL),
        "conv_w": w(ks[10], (L, CONV_K, CONV_CH), CONV_K),
        "conv_b": bias(ks[11], (L, CONV_CH)),
        "conv_ln_g": gain(ks[12], (L, CONV_CH)),
        "conv_ln_b": bias(ks[13], (L, CONV_CH)),
        "conv_w_pw": w(ks[14], (L, CONV_CH, D), CONV_CH),
        "conv_b_pw": bias(ks[15], (L, D)),
        "w_out": w(ks[16], (L, D, D), D),
        "ffn2_norm": gain(ks[17], (L, D)),
        "ffn2_w_gate": w(ks[18], (L, D, F), D),
        "ffn2_w_up": w(ks[19], (L, D, F), D),
        "ffn2_w_down": w(ks[20], (L, F, D), F),
        "final_norm": gain(ks[21], (D,)),
    }


def reference(x, ffn1_norm, ffn1_w_gate, ffn1_w_up, ffn1_w_down, mix_norm, w_in,
              hgrn_lb_logits, hgrn_head_norm, hgrn_w_o, conv_w, conv_b, conv_ln_g, conv_ln_b,
              conv_w_pw, conv_b_pw, w_out, ffn2_norm, ffn2_w_gate, ffn2_w_up, ffn2_w_down,
              final_norm):
    B, L, D = x.shape
    lb_all = jnp.cumsum(jax.nn.softmax(hgrn_lb_logits.astype(jnp.float32), axis=0), axis=0)
    split_idx = list(np.cumsum(IN_SPLITS)[:-1])

    for l in range(DEPTH):
        h = rms_norm(x, ffn1_norm[l])
        x = x + FFN_RES * swiglu_ffn(h, ffn1_w_gate[l], ffn1_w_up[l], ffn1_w_down[l])

        h = rms_norm(x, mix_norm[l])
        proj = h @ w_in[l]
        q, f_logit, i_val, g_out, c_val, c_gate, ga_logit, gb_logit = jnp.split(proj, split_idx, axis=-1)

        lb = lb_all[l]
        f = lb + (1.0 - lb) * jax.nn.sigmoid(f_logit.astype(jnp.float32))
        f = jnp.clip(f, 1e-6, 1.0)
        logf = jnp.log(f).reshape(B, L, HGRN_HEADS, HGRN_DK)
        k = (1.0 - f).reshape(B, L, HGRN_HEADS, HGRN_DK)
        qh = q.astype(jnp.float32).reshape(B, L, HGRN_HEADS, HGRN_DK)
        vh = i_val.astype(jnp.float32).reshape(B, L, HGRN_HEADS, HGRN_DV)
        o = hgrn2_chunkwise(qh, logf, k, vh).astype(x.dtype)
        o = rms_norm(o, hgrn_head_norm[l]).reshape(B, L, HGRN_V_TOTAL)
        y_a = (o * jax.nn.silu(g_out)) @ hgrn_w_o[l]

        u = c_val * jax.nn.sigmoid(c_gate)
        u = causal_depthwise_conv(u, conv_w[l], conv_b[l])
        u = jax.nn.silu(layer_norm(u, conv_ln_g[l], conv_ln_b[l]))
        y_b = u @ conv_w_pw[l] + conv_b_pw[l]

        merged = jax.nn.sigmoid(ga_logit) * y_a + jax.nn.sigmoid(gb_logit) * y_b
        x = x + merged @ w_out[l]

        h = rms_norm(x, ffn2_norm[l])
        x = x + FFN_RES * swiglu_ffn(h, ffn2_w_gate[l], ffn2_w_up[l], ffn2_w_down[l])

    return rms_norm(x, final_norm)
```

```python
import bisect
import numpy as np
import concourse.bass as bass
import concourse.mybir as mybir
from concourse.bass_utils import run_bass_kernel_spmd

F32 = mybir.dt.float32
BF16 = mybir.dt.bfloat16
AF = mybir.ActivationFunctionType
ALU = mybir.AluOpType

NCORES = 8
D = 1024
DFF = 2816
NFC = 22
T = 2048
HALO = 32
TT = T + HALO
EPS = 1e-6
TILES_ALL = [(0, HALO)] + [(HALO + 512 * i, 512) for i in range(4)]
TILES_OWN = TILES_ALL[1:]
FGROUPS = [(0, 4), (4, 4), (8, 4), (12, 4), (16, 4), (20, 2)]

DEBUG_STAGE = "full"
NO_CC = False


class Op:
    __slots__ = ("eng", "fn", "deps", "signal", "ord", "key", "gidx", "unit")

    def __init__(self, eng, fn, key=None, unit=16):
        self.eng = eng
        self.fn = fn
        self.deps = set()
        self.signal = False
        self.ord = 0
        self.key = key
        self.gidx = 0
        self.unit = unit


class Prog:
    ENGS = ("pe", "act", "dve", "pool", "sp")

    def __init__(self):
        self.ops = []
        self.wlast = {}
        self.rlast = {}
        self.pending_bar = {}
        self.last_on_eng = {}
        self.last_dma = {}

    def op(self, eng, fn, r=(), w=(), key=None, unit=16):
        o = Op(eng, fn, key, unit)
        o.gidx = len(self.ops)
        is_dma = key is not None
        deps = set()
        for res in r:
            for p in self.wlast.get(res, {}).values():
                deps.add((p, "raw"))
        for res in w:
            for p in self.wlast.get(res, {}).values():
                deps.add((p, "waw"))
            for p in self.rlast.get(res, {}).values():
                deps.add((p, "war"))
        for (p, kind) in deps:
            if p is o:
                continue
            if p.eng == eng and p.key is None and not is_dma:
                if eng == "pe":
                    continue
            o.deps.add(p)
        if eng in self.pending_bar:
            o.deps |= self.pending_bar.pop(eng)
        best = {}
        keep = set()
        for p in o.deps:
            if p.key is None:
                if p.eng not in best or p.gidx > best[p.eng].gidx:
                    best[p.eng] = p
            else:
                keep.add(p)
        o.deps = keep | set(best.values())
        th = o.eng if key is None else ("dma", key)
        for res in r:
            self.rlast.setdefault(res, {})[th] = o
        for res in w:
            self.wlast.setdefault(res, {})[th] = o
        self.ops.append(o)
        if is_dma:
            self.last_dma[key] = o
        else:
            self.last_on_eng[eng] = o
        return o

    def barrier(self):
        allp = set(self.last_on_eng.values()) | set(self.last_dma.values())
        for e in self.ENGS:
            self.pending_bar[e] = set(allp) | self.pending_bar.get(e, set())

    def emit(self, nc, ehandles, esems, dsems):
        for o in self.ops:
            for p in o.deps:
                if p.key is None:
                    p.signal = True
        cnt = {e: 0 for e in self.ENGS}
        keyidx = {}
        for o in self.ops:
            if o.key is None:
                if o.signal:
                    cnt[o.eng] += 1
                o.ord = cnt[o.eng]
            else:
                keyidx.setdefault(o.key, []).append(o.gidx)
        waited = {}
        streams = {e: [] for e in self.ENGS}
        for o in self.ops:
            waits = {}
            for p in o.deps:
                if p.key is None:
                    sem = ("e", p.eng)
                    val = p.ord
                else:
                    sem = ("d", p.key)
                    val = p.unit * bisect.bisect_left(keyidx[p.key], o.gidx)
                if val > waits.get(sem, 0):
                    waits[sem] = val
            wl = []
            for sem, val in waits.items():
                if waited.get((o.eng, sem), 0) >= val:
                    continue
                waited[(o.eng, sem)] = val
                wl.append((sem, val))
            streams[o.eng].append((o, wl))
        return streams


def run_stream(stream, eh, esems, dsems):
    for (o, wl) in stream:
        for (sem, val) in wl:
            s = esems[sem[1]] if sem[0] == "e" else dsems[sem[1]]
            eh.wait_ge(s, val)
        if o.fn is None:
            continue
        ins = o.fn(eh)
        if o.key is not None:
            if o.unit == 16:
                ins.then_inc(dsems[o.key], 16)
            else:
                ins.then_inc(dsems[o.key], 1)
        elif o.signal:
            ins.then_inc(esems[o.eng], 1)


def build_program():
    nc = bass.Bass("TRN2", target_bir_lowering=False)

    def din(name, shape):
        return nc.dram_tensor(name, list(shape), F32, kind="ExternalInput").ap()

    xT = din("xT", [128, 8 * TT])
    wg_d = [din("wg1", [128, 8, DFF]), din("wg2", [128, 8, DFF])]
    wu_d = [din("wu1", [128, 8, DFF]), din("wu2", [128, 8, DFF])]
    wd_d = [din("wd1", [128, NFC, D]), din("wd2", [128, NFC, D])]
    whg_d = din("whg", [8, 128, 8, 512])
    wcv_d = din("wcv", [8, 128, 8, 256])
    wgab_d = din("wgab", [8, 128, 8, 256])
    wo_d = din("wo", [8, 128, 8, 128])
    wpw_d = din("wpw", [8, 128, 8, 128])
    wout_d = din("wout", [8, 128, 8, 128])
    NSM = 8 * 13 + 2 + 8 * 31 + 2 + 24
    small_d = din("small", [128, NSM])
    cst_d = din("cst", [128, 128 * 2 + 512 * 3])
    inc_d = din("inc", [128, 16])
    y_d = nc.dram_tensor("y", [128, 8, T], F32, kind="ExternalOutput").ap()
    xsp_d = nc.dram_tensor("xsp", [128, 8 * TT], F32, kind="Internal").ap()
    GW = 8 * 128 + 8
    gsrc_d = nc.dram_tensor("gsrc", [128, GW], F32, kind="Internal").ap()
    gdst_d = nc.dram_tensor("gdst", [4 * 128, GW], F32, kind="Internal").ap()

    NW = 52000
    P = Prog()

    arena_cm = nc.sbuf_tensor("arena", [128, NW], F32)
    arena_t = arena_cm.__enter__()
    AR = arena_t[:]
    ps_cms = [nc.psum_tensor("ps%d" % i, [128, 512], F32) for i in range(8)]
    PS = [cm.__enter__()[:] for cm in ps_cms]

    def f32v(off, n):
        return AR[:, off:off + n]

    def bfv(off, nwords):
        return AR[:, off:off + nwords].bitcast(BF16)

    X_OFF = 0
    XW = 8 * TT
    H_OFF = X_OFF + XW
    HW = 8 * TT // 2
    C_OFF = H_OFF + HW
    P_OFF = C_OFF + 4224
    X3 = f32v(X_OFF, XW).rearrange("p (c t) -> p c t", c=8)
    H3 = bfv(H_OFF, HW).rearrange("p (c t) -> p c t", c=8)

    co = [C_OFF]

    def calloc(n):
        o = co[0]
        co[0] += n
        assert co[0] <= P_OFF
        return o

    SM = f32v(calloc(NSM), NSM)
    G1, G2, G3, GF = SM[:, 0:8], SM[:, 8:16], SM[:, 16:24], SM[:, 24:32]
    LB0, LB1 = SM[:, 32:40], SM[:, 40:48]
    CONVB, LNG, LNB, PWB = SM[:, 48:56], SM[:, 56:64], SM[:, 64:72], SM[:, 72:80]
    LB, OML = SM[:, 80:88], SM[:, 88:96]
    HN = SM[:, 96:97]
    LBD = SM[:, 98:106]
    CONVW = SM[:, 106:106 + 248].rearrange("p (c k) -> p c k", c=8)
    INC = f32v(calloc(16), 16)
    IDENT = bfv(calloc(64), 64)
    ONES = bfv(calloc(64), 64)
    CMASK4 = f32v(calloc(512), 512)
    RESET = f32v(calloc(512), 512)
    ONES32 = f32v(calloc(512), 512)
    RSTD = f32v(calloc(TT), TT)
    BTOT = f32v(calloc(8), 8)

    def ffn_map():
        o = P_OFF
        m = {}
        m["wg"] = [bfv(o, 2048).rearrange("p (c f) -> p c f", c=8), bfv(o + 2048, 2048).rearrange("p (c f) -> p c f", c=8)]
        o += 4096
        m["wu"] = [bfv(o, 2048).rearrange("p (c f) -> p c f", c=8), bfv(o + 2048, 2048).rearrange("p (c f) -> p c f", c=8)]
        o += 4096
        m["wd"] = [bfv(o, 2048).rearrange("p (j d) -> p j d", j=4), bfv(o + 2048, 2048).rearrange("p (j d) -> p j d", j=4)]
        o += 4096
        m["a"] = [[bfv(o + (ab * 4 + j) * 256, 256) for j in range(4)] for ab in range(2)]
        o += 2048
        m["sl"] = [f32v(o, 512), f32v(o + 512, 512)]
        o += 1024
        m["sq"] = [bfv(o + i * 256, 256) for i in range(4)]
        o += 1024
        assert o <= NW
        m["ost"] = [f32v(P_OFF, 4096).rearrange("p (c t) -> p c t", c=8), f32v(P_OFF + 4096, 4096).rearrange("p (c t) -> p c t", c=8)]
        return m

    FM = ffn_map()

    def dma(eng, out, in_, r, w, key):
        P.op(eng, lambda e, out=out, in_=in_: e.dma_start(out=out, in_=in_), r=r, w=w, key=key)

    def mm(out, lhsT, rhs, start, stop, r, w):
        P.op("pe", lambda e, out=out, lhsT=lhsT, rhs=rhs, start=start, stop=stop:
             e.matmul(out, lhsT, rhs, start=start, stop=stop), r=r, w=w)

    def act(out, in_, func, r, w, bias=None, scale=None):
        def fn(e, out=out, in_=in_, func=func, bias=bias, scale=scale):
            kw = {}
            if bias is not None:
                kw["bias"] = bias
            if scale is not None:
                kw["scale"] = scale
            return e.activation(out, in_, func, **kw)
        P.op("act", fn, r=r, w=w)

    def tt(eng, out, in0, in1, op, r, w):
        P.op(eng, lambda e, out=out, in0=in0, in1=in1, op=op: e.tensor_tensor(out, in0, in1, op), r=r, w=w)

    def ts(eng, out, in0, s1, s2, op0, op1, r, w):
        if op1 is None:
            P.op(eng, lambda e, out=out, in0=in0, s1=s1, op0=op0: e.tensor_scalar(out, in0, s1, None, op0), r=r, w=w)
        else:
            P.op(eng, lambda e, out=out, in0=in0, s1=s1, s2=s2, op0=op0, op1=op1:
                 e.tensor_scalar(out, in0, s1, s2, op0, op1), r=r, w=w)

    def stt(out, in0, scalar, in1, op0, op1, r, w):
        P.op("dve", lambda e, out=out, in0=in0, scalar=scalar, in1=in1, op0=op0, op1=op1:
             e.scalar_tensor_tensor(out, in0, scalar, in1, op0, op1), r=r, w=w)

    def cp(eng, out, in_, r, w):
        if eng == "act":
            act(out, in_, AF.Copy, r, w)
        else:
            P.op(eng, lambda e, out=out, in_=in_: e.tensor_copy(out, in_), r=r, w=w)

    def xr(c, ti):
        return ("X", c, ti)

    def hr(c, ti):
        return ("H", c, ti)

    dma("sp", SM, small_d, r=[], w=["SM"], key="cst")
    dma("sp", INC, inc_d, r=[], w=["INC"], key="cst")
    dma("pool", ONES, cst_d[:, 128:256], r=[], w=["ONES"], key="cstb")
    xT3 = xT.rearrange("p (c t) -> p c t", c=8)
    def load_x_tile(ti, extra_r=()):
        t0, w = TILES_ALL[ti]
        dma("sp", X3[:, :, t0:t0 + w], xT3[:, :, t0:t0 + w], r=list(extra_r), w=[xr(c, ti) for c in range(8)], key=("xin", ti))

    load_x_tile(0)
    load_x_tile(1)
    DEFER_X = [True]
    dma("pool", IDENT, cst_d[:, 0:128], r=[], w=["IDENT"], key="cstb2")
    dma("sp", CMASK4, cst_d[:, 256:768], r=[], w=["CMASK"], key="cst2")
    dma("sp", RESET, cst_d[:, 768:1280], r=[], w=["RESET"], key="cst2")
    dma("sp", ONES32, cst_d[:, 1280:1792], r=[], w=["ONES32"], key="cst2")
    tt("dve", LBD, LB0, LB1, ALU.subtract, r=["SM"], w=["LBD"])
    act(LB, LBD, AF.Sigmoid, r=["LBD"], w=["LB"])
    ts("dve", OML, LB, -1.0, 1.0, ALU.mult, ALU.add, r=["LB"], w=["OML"])
    SMIN = SM[:, 356:364]
    NOML = SM[:, 364:372]
    ROML = SM[:, 372:380]
    ts("dve", NOML, OML, -1.0, None, ALU.mult, None, r=["OML"], w=["NOML"])
    P.op("dve", lambda e: e.reciprocal(ROML, OML), r=["OML"], w=["ROML"])
    ts("dve", SMIN, LB, -1.0, 1e-6, ALU.mult, ALU.add, r=["LB"], w=["SMIN0"])
    tt("dve", SMIN, SMIN, ROML, ALU.mult, r=["SMIN0", "ROML"], w=["SMIN"])

    def rms_norm(gcols, tiles, tile_ids):
        for (t0, w), ti in zip(tiles, tile_ids):
            norm_tile(gcols, t0, w, ti)

    def norm_tile(gcols, t0, w, ti):
        if True:
            pn = PS[7]
            for c in range(8):
                sq = FM["sq"][c % 4]
                act(sq[:, :w], X3[:, c, t0:t0 + w], AF.Square, r=[xr(c, ti)], w=[("sq", c % 4)])
                mm(pn[:, :w], ONES, sq[:, :w], c == 0, c == 7, r=["ONES", ("sq", c % 4)], w=["ps7"])
            rstd_act(RSTD[:, t0:t0 + w], pn[:, :w], 1.0 / D, r=["ps7"], w=[("rstd", ti)])
            for c in range(8):
                stt(H3[:, c, t0:t0 + w], X3[:, c, t0:t0 + w], gcols[:, c:c + 1], RSTD[:, t0:t0 + w],
                    ALU.mult, ALU.mult, r=[xr(c, ti), ("rstd", ti), "SM"], w=[hr(c, ti)])

    EPSB = SM[:, 97:98]
    ONEB = SM[:, 354:355]

    def rstd_act(out, in_, scale, r, w):
        act(out, in_, AF.Ln, r=r, w=w, bias=EPSB, scale=scale)
        act(out, out, AF.Exp, r=w, w=w, scale=-0.5)

    def sigmoid_act(out, in_, r, w):
        act(out, in_, AF.Exp, r=r, w=w, scale=-1.0)
        act(out, out, AF.Ln, r=w, w=w, bias=ONEB, scale=1.0)
        act(out, out, AF.Exp, r=w, w=w, scale=-1.0)

    PRELOADED = set()
    WOVR = {}

    def ffn_load_gu(which, gi, extra_r=()):
        j0, nf = FGROUPS[gi]
        s = gi % 2
        dma("pool", FM["wg"][s][:, :, :nf * 128], wg_d[which][:, :, j0 * 128:(j0 + nf) * 128], r=list(extra_r), w=[("wg", s)], key=("wg", s))
        dma("pool", FM["wu"][s][:, :, :nf * 128], wu_d[which][:, :, j0 * 128:(j0 + nf) * 128], r=list(extra_r), w=[("wu", s)], key=("wu", s))

    def ffn(which, tiles, tile_ids, gcols):
        steps = [(gi, k) for gi in range(len(FGROUPS)) for k in range(len(tiles))]

        def load(gi):
            j0, nf = FGROUPS[gi]
            s = gi % 2
            if (which, gi) not in PRELOADED and (which, gi) not in WOVR:
                ffn_load_gu(which, gi)
            dma("pool", FM["wd"][s][:, :nf, :], wd_d[which][:, j0:j0 + nf, :], r=[], w=[("wd", s)], key=("wd", s))

        def gu(si):
            gi, k = steps[si]
            j0, nf = FGROUPS[gi]
            s = gi % 2
            ab = si % 2
            (t0, w), ti = tiles[k], tile_ids[k]
            if (which, gi) in WOVR:
                WG, WU, rg, ru = WOVR[(which, gi)]
            else:
                WG, WU, rg, ru = FM["wg"][s], FM["wu"][s], ("wg", s), ("wu", s)
            for j in range(nf):
                pg, pu = PS[j % 2], PS[2 + j % 2]
                for c in range(8):
                    mm(pg[:, :w], WG[:, c, j * 128:(j + 1) * 128], H3[:, c, t0:t0 + w], c == 0, c == 7,
                       r=[rg, hr(c, ti)], w=["ps%d" % (j % 2)])
                for c in range(8):
                    mm(pu[:, :w], WU[:, c, j * 128:(j + 1) * 128], H3[:, c, t0:t0 + w], c == 0, c == 7,
                       r=[ru, hr(c, ti)], w=["ps%d" % (2 + j % 2)])
                sl = FM["sl"][j % 2]
                act(sl[:, :w], pg[:, :w], AF.Silu, r=["ps%d" % (j % 2)], w=[("sl", j % 2)])
                tt("dve", FM["a"][ab][j][:, :w], pu[:, :w], sl[:, :w], ALU.mult,
                   r=["ps%d" % (2 + j % 2), ("sl", j % 2)], w=[("a", ab, j)])

        def dn(si):
            gi, k = steps[si]
            j0, nf = FGROUPS[gi]
            s = gi % 2
            ab = si % 2
            (t0, w), ti = tiles[k], tile_ids[k]
            for o in range(8):
                pd = PS[4 + o % 4]
                for j in range(nf):
                    mm(pd[:, :w], FM["wd"][s][:, j, o * 128:(o + 1) * 128], FM["a"][ab][j][:, :w], j == 0, j == nf - 1,
                       r=[("wd", s), ("a", ab, j)], w=["ps%d" % (4 + o % 4)])
                stt(X3[:, o, t0:t0 + w], pd[:, :w], 0.5, X3[:, o, t0:t0 + w], ALU.mult, ALU.add,
                    r=["ps%d" % (4 + o % 4), xr(o, ti)], w=[xr(o, ti)])

        load(0)
        if which == 0 and DEFER_X[0]:
            DEFER_X[0] = False
            for ti_ in range(2, 5):
                load_x_tile(ti_, extra_r=[("wu", 0)])
        for si in range(len(steps)):
            if steps[si][0] == 0:
                k0 = steps[si][1]
                norm_tile(gcols, tiles[k0][0], tiles[k0][1], tile_ids[k0])
            gu(si)
            if si == 0 and len(FGROUPS) > 1:
                load(1)
            if si > 0:
                dn(si - 1)
                gi_prev, k_prev = steps[si - 1]
                if k_prev == len(tiles) - 1 and gi_prev + 2 < len(FGROUPS):
                    load(gi_prev + 2)
        dn(len(steps) - 1)

    ALL_IDS = list(range(5))
    OWN_IDS = list(range(1, 5))
    ffn(0, TILES_ALL, ALL_IDS, G1)

    out_cnt = [0]

    def dump_X():
        P.barrier()
        for c in range(8):
            dma("sp", y_d[:, c, :], X3[:, c, HALO:HALO + T], r=[xr(c, ti) for ti in range(5)], w=[("y", c)], key="out")
            out_cnt[0] += 1

    def finish():
        P.op("sp", None, r=[("y", c) for c in range(8)] + ["yfin"], w=[])
        P.barrier()

    if DEBUG_STAGE == "ffn1":
        dump_X()
        P.op("sp", None, r=[("y", c) for c in range(8)], w=[])
        return nc, P, (arena_cm, ps_cms)

    rms_norm(G2, TILES_ALL, ALL_IDS)
    for c in range(8):
        dma("sp", xsp_d[:, c * TT:(c + 1) * TT], X3[:, c, :], r=[xr(c, ti) for ti in range(5)], w=[("xsp", c)], key=("xsp", c))
    P.barrier()

    A_OFF = P_OFF
    B_OFF = P_OFF + 8192
    R_OFF = P_OFF + 16384
    A3 = bfv(A_OFF, 8192).rearrange("p (c t) -> p c t", c=8)
    B3 = bfv(B_OFF, 8192).rearrange("p (c t) -> p c t", c=8)

    xo = [X_OFF]

    def xalloc(n):
        o = xo[0]
        xo[0] += n
        assert xo[0] <= X_OFF + XW, xo[0]
        return o

    ro = [R_OFF]

    def ralloc(n):
        o = ro[0]
        ro[0] += n
        assert ro[0] <= NW, ro[0]
        return o

    WH = [bfv(xalloc(2048), 2048).rearrange("p (c f) -> p c f", c=8) for _ in range(2)]
    TF = [[f32v(xalloc(512), 512) for _ in range(5)] for _ in range(2)]
    TB = [[bfv(xalloc(256), 256) for _ in range(6)] for _ in range(3)]
    SALL = [f32v(xalloc(1024), 1024).rearrange("p (n d) -> p n d", n=8) for _ in range(2)]
    GS = f32v(ralloc(GW), GW)
    GS3 = GS[:, 0:1024].rearrange("p (h d) -> p h d", h=8)
    BT = f32v(ralloc(64), 64)
    BT3 = BT.rearrange("p (r h) -> p r h", r=8)
    NR = 4
    EB_ = f32v(ralloc(64), 64)
    CR = f32v(ralloc(64), 64)
    RS = f32v(ralloc(1024), 1024)
    RS3 = RS.rearrange("p (h d) -> p h d", h=8)
    GL = [f32v(ralloc(1024), 1024) for _ in range(2)]
    SIN = bfv(ralloc(512), 512).rearrange("p (h d) -> p h d", h=8)
    SBF = [bfv(ralloc(512), 512).rearrange("p (n d) -> p n d", n=8) for _ in range(2)]
    EBL = [f32v(ralloc(8), 8) for _ in range(3)]

    pq, pf, pv, ptr, pA, pU0, pU1, po = PS
    ptr_bf = ptr.bitcast(BF16)
    NT = 32

    def tinfo(k):
        hd, ti4 = k // 4, k % 4
        return hd, ti4, ti4 + 1, HALO + 512 * ti4, 512 * ti4

    def load_wh(hd):
        s_w = hd % 2
        dma("pool", WH[s_w][:, :, 0:384], whg_d[hd, :, :, 0:384], r=[], w=[("wh", s_w)], key=("wh", s_w))

    def F1(k):
        hd, ti4, ti, t0, to = tinfo(k)
        s_w, f, b = hd % 2, k % 2, k % 3
        if ti4 == 0 and hd + 1 < 8:
            load_wh(hd + 1)
        for c in range(8):
            mm(pq, WH[s_w][:, c, 0:128], H3[:, c, t0:t0 + 512], c == 0, c == 7, r=[("wh", s_w), hr(c, ti)], w=["ps0"])
        for s in range(4):
            for c in range(8):
                mm(pv[:, s * 128:(s + 1) * 128], H3[:, c, t0 + s * 128:t0 + (s + 1) * 128], WH[s_w][:, c, 256:384],
                   c == 0, c == 7, r=[("wh", s_w), hr(c, ti)], w=["ps2"])

    def F1f(k):
        hd, ti4, ti, t0, to = tinfo(k)
        s_w = hd % 2
        for c in range(8):
            mm(pf, WH[s_w][:, c, 128:256], H3[:, c, t0:t0 + 512], c == 0, c == 7, r=[("wh", s_w), hr(c, ti)], w=["ps1"])

    def S1(k):
        f, b = k % 2, k % 3
        T1 = TF[f][0]
        VTOK = TB[b][4]
        sigmoid_act(T1, pf, r=["ps1"], w=[("tf", f, 0)])

    def E0(k):
        hd, ti4, ti, t0, to = tinfo(k)
        f = k % 2
        T1, T2 = TF[f][0], TF[f][1]
        rT = [("tf", f, i) for i in range(5)]
        ts("dve", T1, T1, SMIN[:, hd:hd + 1], None, ALU.max, None, r=[rT[0], "SMIN"], w=[rT[0]])
        act(T2, T1, AF.Ln, r=[rT[0], "LB", "OML"], w=[rT[1]], bias=LB[:, hd:hd + 1], scale=OML[:, hd:hd + 1])

    def E(k):
        hd, ti4, ti, t0, to = tinfo(k)
        f, b = k % 2, k % 3
        fo = 1 - f
        T1, T2, T3, T4, T5 = TF[f]
        QE, KE, KHT, KHTOK, VTOK, AM = TB[b]
        rT = [("tf", f, i) for i in range(5)]
        rB = [("tb", b, i) for i in range(6)]
        P.op("dve", lambda e, o=T3, d1=T2: e.tensor_tensor_scan(o, RESET, d1, 0.0, ALU.mult, ALU.add),
             r=[rT[1], "RESET"], w=[rT[2]])
        if ti4 == 0:
            P.op("dve", lambda e, o=T4, d1=T2: e.tensor_tensor_scan(o, ONES32, d1, 0.0, ALU.mult, ALU.add),
                 r=[rT[1], "ONES32"], w=[rT[3]])
        else:
            prevT4 = TF[fo][3]
            P.op("dve", lambda e, o=T4, d1=T2, ini=prevT4[:, 511:512]: e.tensor_tensor_scan(o, ONES32, d1, ini, ALU.mult, ALU.add),
                 r=[rT[1], "ONES32", ("tf", fo, 3)], w=[rT[3]])
        ts("pool", T1, T1, NOML[:, hd:hd + 1], OML[:, hd:hd + 1], ALU.mult, ALU.add, r=[rT[0], "NOML", "OML"], w=[rT[0]])

    def Ec1(k):
        f = k % 2
        T1, T2, T3, T4, T5 = TF[f]
        rT = [("tf", f, i) for i in range(5)]
        act(T5, T3, AF.Exp, r=[rT[2]], w=[rT[4]])

    def Ec(k):
        f = k % 2
        T1, T2, T3, T4, T5 = TF[f]
        rT = [("tf", f, i) for i in range(5)]
        act(T2, T4, AF.Exp, r=[rT[3], rT[1]], w=[rT[1]])
        act(T3, T3, AF.Exp, r=[rT[2]], w=[rT[2]], scale=-1.0)

    def Eb(k):
        hd, ti4, ti, t0, to = tinfo(k)
        f, b = k % 2, k % 3
        T1, T2, T3, T4, T5 = TF[f]
        QE, KE, KHT, KHTOK, VTOK, AM = TB[b]
        rT = [("tf", f, i) for i in range(5)]
        rB = [("tb", b, i) for i in range(6)]
        cp("act", VTOK, pv, r=["ps2"], w=[("tb", b, 4)])
        tt("dve", QE, pq, T5, ALU.mult, r=["ps0", rT[4]], w=[rB[0]])
        tt("dve", B3[:, hd, to:to + 512], pq, T2, ALU.mult, r=["ps0", rT[1]], w=[("B", hd, ti4)])
        tt("pool", KE, T1, T3, ALU.mult, r=[rT[0], rT[2]], w=[rB[1]])
        tt("pool", KHT.rearrange("p (n t) -> p n t", n=8), KE.rearrange("p (n t) -> p n t", n=8),
           T5[:, 63:512:64].to_broadcast([128, 8, 64]), ALU.mult, r=[rB[1], rT[4]], w=[rB[2]])
        cp("pool", EBL[b], T5[:, 63:512:64], r=[rT[4]], w=[("ebl", b)])
        if ti4 == 3:
            cp("pool", GS[:, 1024 + hd:1025 + hd], T4[:, 511:512], r=[rT[3]], w=["GS"])

    def F2(k):
        b = k % 3
        QE, KE, KHT, KHTOK, VTOK, AM = TB[b]
        rB = [("tb", b, i) for i in range(6)]
        for s in range(4):
            P.op("pe", lambda e, o=ptr_bf[:, s * 128:(s + 1) * 128], i=KHT[:, s * 128:(s + 1) * 128]: e.transpose(o, i, IDENT),
                 r=[rB[2], "IDENT"], w=["ps3"])
        cp("act", KHTOK, ptr_bf[:, 0:512], r=["ps3"], w=[rB[3]])
        for s in range(4):
            mm(pA[:, s * 128:(s + 1) * 128], KE[:, s * 128:(s + 1) * 128], QE[:, s * 128:(s + 1) * 128], True, True,
               r=[rB[0], rB[1]], w=["ps4"])
        tt("dve", AM, pA, CMASK4, ALU.mult, r=["ps4", "CMASK"], w=[rB[5]])
        for n in range(8):
            s, half = n // 2, n % 2
            r0 = half * 64
            pu = pU0 if half == 0 else pU1
            mm(pu[:, s * 128:(s + 1) * 128], KHTOK[r0:r0 + 64, s * 128:(s + 1) * 128],
               VTOK[r0:r0 + 64, s * 128:(s + 1) * 128], True, True, r=[rB[3], rB[4]], w=["ps%d" % (5 + half)])

    def Bk(k):
        hd, ti4, ti, t0, to = tinfo(k)
        b, st = k % 3, k % 2
        so = 1 - st
        QE, KE, KHT, KHTOK, VTOK, AM = TB[b]
        rB = [("tb", b, i) for i in range(6)]
        for n in range(8):
            pu = pU0 if n % 2 == 0 else pU1
            pun = pu[:, (n // 2) * 128:(n // 2 + 1) * 128]
            if n == 0 and ti4 == 0:
                cp("dve", SALL[st][:, 0, :], pun, r=["ps5"], w=[("sall", st)])
            else:
                prev = SALL[st][:, n - 1, :] if n > 0 else SALL[so][:, 7, :]
                rr = [("sall", st), "ps%d" % (5 + n % 2), ("ebl", b)] + ([("sall", so)] if n == 0 else [])
                stt(SALL[st][:, n, :], prev, EBL[b][:, n:n + 1], pun, ALU.mult, ALU.add, r=rr, w=[("sall", st)])
        if ti4 == 0:
            P.op("pool", lambda e, o=SBF[st][:, 0, :]: e.memset(o, 0.0), r=[], w=[("sbf", st)])
        else:
            cp("pool", SBF[st][:, 0, :], SALL[so][:, 7, :], r=[("sall", so)], w=[("sbf", st)])
        if ti4 == 3:
            cp("pool", GS3[:, hd, :], SALL[st][:, 7, :], r=[("sall", st)], w=["GS"])

    def Bcast(k):
        st = k % 2
        cp("act", SBF[st][:, 1:8, :], SALL[st][:, 0:7, :], r=[("sall", st)], w=[("sbf", st)])

    def Bo(k):
        hd, ti4, ti, t0, to = tinfo(k)
        b, st = k % 3, k % 2
        QE, KE, KHT, KHTOK, VTOK, AM = TB[b]
        rB = [("tb", b, i) for i in range(6)]
        for s in range(4):
            mm(po[:, s * 128:(s + 1) * 128], VTOK[:, s * 128:(s + 1) * 128], AM[:, s * 128:(s + 1) * 128], True, False,
               r=[rB[4], rB[5]], w=["ps7"])
            mm(po[:, s * 128:s * 128 + 64], SBF[st][:, 2 * s, :], QE[:, s * 128:s * 128 + 64], False, False,
               r=[("sbf", st), rB[0]], w=["ps7"])
            mm(po[:, s * 128 + 64:s * 128 + 128], SBF[st][:, 2 * s + 1, :], QE[:, s * 128 + 64:s * 128 + 128], False, True,
               r=[("sbf", st), rB[0]], w=["ps7"])
        cp("act", A3[:, hd, to:to + 512], po, r=["ps7"], w=[("A", hd, ti4)])

    load_wh(0)
    F1f(0)
    S1(0)
    for step in range(NT + 2):
        if step < NT:
            F1(step)
            if step + 1 < NT:
                F1f(step + 1)
            E0(step)
        if 0 <= step - 2 < NT:
            Bk(step - 2)
            Bcast(step - 2)
        if step < NT:
            E(step)
            Ec1(step)
        if 0 <= step - 2 < NT:
            Bo(step - 2)
        if 0 <= step - 1 < NT:
            F2(step - 1)
        if step < NT:
            Ec(step)
            Eb(step)
            if step + 1 < NT:
                S1(step + 1)

    P.barrier()
    dma("sp", gsrc_d, GS, r=["GS"], w=["gsrc"], key="gs")
    if NO_CC:
        dma("sp", gdst_d[0:128, :], gsrc_d, r=["gsrc"], w=["gdst"], key="cc")
    else:
        P.op("pool", lambda e: e.collective_compute("AllGather", ALU.bypass, replica_groups=[[0, 1, 2, 3], [4, 5, 6, 7]],
                                                    ins=[gsrc_d], outs=[gdst_d]),
             r=["gsrc"], w=["gdst"], key="cc", unit=1)

    CW_OFF = X_OFF
    WSC = [bfv(CW_OFF + i * 1024, 1024).rearrange("p (c f) -> p c f", c=8) for i in range(2)]
    DG = [bfv(CW_OFF + 2048 + 64 * k, 64) for k in range(31)]
    UB = [bfv(CW_OFF + 4032 + i * (TT // 2), TT // 2) for i in range(2)]
    TC = [f32v(CW_OFF + 6112 + i * 512, 512) for i in range(2)]
    NDV = 3
    C_OFF3 = CW_OFF + 7136
    assert C_OFF3 + 8192 <= X_OFF + XW
    C3 = bfv(C_OFF3, 8192).rearrange("p (c t) -> p c t", c=8)
    def emit_combine():
        gd3 = gdst_d.rearrange("(r p) c -> p r c", p=128)
        for r_ in range(NR):
            dma("sp", BT3[:, r_, :], gd3[:, r_, 1024:1032], r=["gdst"], w=["BT"], key="bt")
        act(EB_[:, 0:8 * NR], BT[:, 0:8 * NR], AF.Exp, r=["BT"], w=["EB_"])
        P.op("dve", lambda e: e.memset(RS, 0.0), r=[], w=["RS"])
        for r_ in range(NR):
            gl = GL[r_ % 2]
            dma("sp", gl, gd3[:, r_, 0:1024], r=["gdst"], w=[("gl", r_ % 2)], key=("gl", r_ % 2))
            ts("dve", CR[:, r_ * 8:(r_ + 1) * 8], EB_[:, r_ * 8:(r_ + 1) * 8], INC[:, r_:r_ + 1], INC[:, 8 + r_:9 + r_],
               ALU.mult, ALU.add, r=["EB_", "INC"], w=[("cr", r_)])
            ts("dve", gl, gl, INC[:, r_:r_ + 1], None, ALU.mult, None, r=[("gl", r_ % 2), "INC"], w=[("gl", r_ % 2)])
            for h in range(8):
                stt(RS3[:, h, :], RS3[:, h, :], CR[:, r_ * 8 + h:r_ * 8 + h + 1], gl[:, h * 128:(h + 1) * 128], ALU.mult, ALU.add,
                    r=["RS", ("cr", r_), ("gl", r_ % 2)], w=["RS"])
        cp("act", SIN, RS3, r=["RS"], w=["SIN"])


    for cc in range(8):
        sw = cc % 2
        if cc == 5:
            emit_combine()
        dma("pool", WSC[sw], wcv_d[cc], r=[], w=[("wsc", sw)], key=("wsc", sw))
        for k in range(31 - NDV):
            ts("dve", DG[k], IDENT, CONVW[:, cc, k:k + 1], None, ALU.mult, None, r=["IDENT", "SM"], w=[("dg", k)])
        for k5, ((t0, w), ti) in enumerate(zip(TILES_ALL, ALL_IDS)):
            st = (cc * 5 + k5) % 2
            T1 = TC[st]
            pcv, pcg = PS[st], PS[2 + st]
            for c in range(8):
                mm(pcv[:, :w], WSC[sw][:, c, 0:128], H3[:, c, t0:t0 + w], c == 0, c == 7, r=[("wsc", sw), hr(c, ti)], w=["ps%d" % st])
            for c in range(8):
                mm(pcg[:, :w], WSC[sw][:, c, 128:256], H3[:, c, t0:t0 + w], c == 0, c == 7, r=[("wsc", sw), hr(c, ti)], w=["ps%d" % (2 + st)])
            act(T1[:, :w], pcg[:, :w], AF.Sigmoid, r=["ps%d" % (2 + st)], w=[("tc", st)])
            tt("dve", UB[sw][:, t0:t0 + w], pcv[:, :w], T1[:, :w], ALU.mult, r=["ps%d" % st, ("tc", st)], w=[("ub", sw, ti)])
        NPE = 31 - NDV
        for ti4 in range(4):
            t0 = HALO + 512 * ti4
            to = 512 * ti4
            pcn = PS[4 + ti4 % 4]
            rr = [("ub", sw, ti4 + 1), ("ub", sw, ti4)]
            acc = TC[ti4 % 2]
            for k in range(NPE, 31):
                src = UB[sw][:, t0 - 30 + k:t0 - 30 + k + 512]
                if k == NPE:
                    ts("dve", acc, src, CONVW[:, cc, k:k + 1], None, ALU.mult, None, r=rr[:2] + ["SM"], w=[("tc", ti4 % 2)])
                else:
                    stt(acc, src, CONVW[:, cc, k:k + 1], acc, ALU.mult, ALU.add, r=rr[:2] + ["SM", ("tc", ti4 % 2)], w=[("tc", ti4 % 2)])
            for k in range(NPE):
                mm(pcn, DG[k], UB[sw][:, t0 - 30 + k:t0 - 30 + k + 512], k == 0, k == NPE - 1, r=rr + [("dg", k)], w=["ps%d" % (4 + ti4 % 4)])
            stt(C3[:, cc, to:to + 512], pcn, CONVB[:, cc:cc + 1], acc, ALU.add, ALU.add,
                r=["ps%d" % (4 + ti4 % 4), "SM", ("tc", ti4 % 2)], w=[("C", cc, ti4)])
    P.barrier()

    ZW_OFF = CW_OFF
    TZ = [[f32v(ZW_OFF + (i * 3 + j) * 512, 512) for j in range(3)] for i in range(2)]
    SQZ = [bfv(ZW_OFF + 3072 + i * 256, 256) for i in range(2)]
    WHG = [bfv(ZW_OFF + 3584 + i * 512, 512).rearrange("p (c f) -> p c f", c=8) for i in range(2)]
    WSA = [bfv(ZW_OFF + 4608 + i * 512, 512).rearrange("p (c f) -> p c f", c=8) for i in range(2)]
    WSB = [bfv(ZW_OFF + 5632 + i * 512, 512).rearrange("p (c f) -> p c f", c=8) for i in range(2)]
    assert ZW_OFF + 6656 <= C_OFF3
    WSD = [bfv(C_OFF3 + 8192 + i * 512, 512).rearrange("p (c f) -> p c f", c=8) for i in range(2)]
    assert C_OFF3 + 8192 + 1024 <= X_OFF + XW

    def rstd_from(out, in_, scale, r, w):
        act(out, in_, AF.Ln, r=r, w=w, bias=EPSB, scale=scale)
        act(out, out, AF.Exp, r=w, w=w, scale=-0.5)

    T1Z = [TZ[0][0], TZ[1][0], f32v(R_OFF + 5856, 512)]
    assert R_OFF + 5856 + 512 <= NW
    T3Z = [TZ[0][2], TZ[1][2], f32v(X_OFF + XW - 512, 512)]
    assert C_OFF3 + 8192 + 1024 <= X_OFF + XW - 512 or True

    def zinfo(k):
        hd, ti4 = k // 4, k % 4
        return hd, ti4, ti4 + 1, HALO + 512 * ti4, 512 * ti4

    def Z1(k):
        hd, ti4, ti, t0, to = zinfo(k)
        s_w = hd % 2
        if ti4 == 0:
            dma("pool", WHG[s_w], whg_d[hd, :, :, 384:512], r=[], w=[("whg", s_w)], key=("whg", s_w))
        pc, pg = PS[k % 2], PS[4 + k % 3]
        mm(pc, SIN[:, hd, :], B3[:, hd, to:to + 512], True, True, r=["SIN", ("B", hd, ti4)], w=["ps%d" % (k % 2)])
        for c in range(8):
            mm(pg, WHG[s_w][:, c, :], H3[:, c, t0:t0 + 512], c == 0, c == 7, r=[("whg", s_w), hr(c, ti)], w=["ps%d" % (4 + k % 3)])
        tt("dve", T1Z[k % 3], pc, A3[:, hd, to:to + 512], ALU.add, r=["ps%d" % (k % 2), ("A", hd, ti4)], w=[("t1z", k % 3)])

    def Z2(k):
        st = k % 2
        T3 = T3Z[k % 3]
        pg = PS[4 + k % 3]
        act(SQZ[st], T1Z[k % 3], AF.Square, r=[("t1z", k % 3)], w=[("sqz", st)])
        sigmoid_act(T3, pg, r=["ps%d" % (4 + k % 3)], w=[("t3z", k % 3)])
        tt("dve", T3, pg, T3, ALU.mult, r=["ps%d" % (4 + k % 3), ("t3z", k % 3)], w=[("t3z", k % 3)])

    def Z3(k):
        hd, ti4, ti, t0, to = zinfo(k)
        st = k % 2
        T2, T3 = TZ[st][1], T3Z[k % 3]
        T1 = T1Z[k % 3]
        pn = PS[2 + st]
        mm(pn, ONES, SQZ[st], True, True, r=["ONES", ("sqz", st)], w=["ps%d" % (2 + st)])
        rstd_act(T2, pn, 1.0 / 128, r=["ps%d" % (2 + st)], w=[("tz", st, 1)])
        stt(T1, T1, HN, T2, ALU.mult, ALU.mult, r=[("t1z", k % 3), ("tz", st, 1), "SM"], w=[("t1z", k % 3)])
        tt("dve", A3[:, hd, to:to + 512], T1, T3, ALU.mult, r=[("t1z", k % 3), ("t3z", k % 3)], w=[("A", hd, ti4)])

    for i in range(34):
        if i < 32:
            Z1(i)
        if 0 <= i - 1 < 32:
            Z2(i - 1)
        if 0 <= i - 2 < 32:
            Z3(i - 2)
    P.barrier()

    MEAN = f32v(R_OFF, T)
    LRS = f32v(R_OFF + T, T)
    for ti4 in range(4):
        to = 512 * ti4
        st = ti4 % 2
        pss, psq = PS[st], PS[2 + st]
        for cc in range(8):
            mm(pss, ONES, C3[:, cc, to:to + 512], cc == 0, cc == 7, r=["ONES", ("C", cc, ti4)], w=["ps%d" % st])
        for cc in range(8):
            sq = SQZ[cc % 2]
            act(sq, C3[:, cc, to:to + 512], AF.Square, r=[("C", cc, ti4)], w=[("sqz", cc % 2)])
            mm(psq, ONES, sq, cc == 0, cc == 7, r=["ONES", ("sqz", cc % 2)], w=["ps%d" % (2 + st)])
        T1 = TZ[st][0]
        act(MEAN[:, to:to + 512], pss, AF.Copy, r=["ps%d" % st], w=[("mean", ti4)], scale=1.0 / D)
        tt("dve", T1, MEAN[:, to:to + 512], MEAN[:, to:to + 512], ALU.mult, r=[("mean", ti4)], w=[("tz", st, 0)])
        stt(T1, psq, 1.0 / D, T1, ALU.mult, ALU.subtract, r=["ps%d" % (2 + st), ("tz", st, 0)], w=[("tz", st, 0)])
        act(LRS[:, to:to + 512], T1, AF.Ln, r=[("tz", st, 0)], w=[("lrs", ti4)], bias=EPSB, scale=1.0)
        act(LRS[:, to:to + 512], LRS[:, to:to + 512], AF.Exp, r=[("lrs", ti4)], w=[("lrs", ti4)], scale=-0.5)
    def ln_apply(ti4):
        to = 512 * ti4
        for cc in range(8):
            st = cc % 2
            T1 = TZ[st][2]
            tt("dve", T1, C3[:, cc, to:to + 512], MEAN[:, to:to + 512], ALU.subtract, r=[("C", cc, ti4), ("mean", ti4)], w=[("tz", st, 2)])
            tt("pool", T1, T1, LRS[:, to:to + 512], ALU.mult, r=[("tz", st, 2), ("lrs", ti4)], w=[("tz", st, 2)])
            act(C3[:, cc, to:to + 512], T1, AF.Silu, r=[("tz", st, 2), "SM"], w=[("C", cc, ti4)],
                bias=LNB[:, cc:cc + 1], scale=LNG[:, cc:cc + 1])

    def load_merged(o):
        sw = o % 2
        dma("pool", WSA[sw], wo_d[o], r=[], w=[("wsa", sw)], key=("wsa", sw))
        dma("pool", WHG[sw], wpw_d[o], r=[], w=[("whg", sw)], key=("whg", sw))
        dma("pool", WSB[sw], wgab_d[o][:, :, 0:128], r=[], w=[("wsb", sw)], key=("wsb", sw))
        dma("pool", WSD[sw], wgab_d[o][:, :, 128:256], r=[], w=[("wsd", sw)], key=("wsd", sw))

    load_merged(0)
    load_merged(1)
    WG2 = bfv(R_OFF, 2048).rearrange("p (c f) -> p c f", c=8)
    WU2 = bfv(R_OFF + 2048, 2048).rearrange("p (c f) -> p c f", c=8)
    lnres = [("mean", t_) for t_ in range(4)] + [("lrs", t_) for t_ in range(4)]
    WOVR[(1, 0)] = (WG2, WU2, "wg2", "wu2")
    for o in range(8):
        sw = o % 2
        if o == 1:
            dma("pool", WG2, wg_d[1][:, :, 0:512], r=[], w=["wg2"] + lnres, key="wg2")
            dma("pool", WU2, wu_d[1][:, :, 0:512], r=[], w=["wu2"] + lnres, key="wu2")
        if 1 <= o and o + 1 < 8:
            load_merged(o + 1)
        for ti4 in range(4):
            if o == 0:
                ln_apply(ti4)
            st = (o * 4 + ti4) % 2
            ti = ti4 + 1
            t0 = HALO + 512 * ti4
            to = 512 * ti4
            T1, T2, T3 = TZ[st]
            pya, pga, pyb, pgb = PS[st], PS[2 + st], PS[4 + st], PS[6 + st]
            for h in range(8):
                mm(pya, WSA[sw][:, h, :], A3[:, h, to:to + 512], h == 0, h == 7, r=[("wsa", sw), ("A", h, ti4)], w=["ps%d" % st])
            for c in range(8):
                mm(pga, WSB[sw][:, c, :], H3[:, c, t0:t0 + 512], c == 0, c == 7, r=[("wsb", sw), hr(c, ti)], w=["ps%d" % (2 + st)])
            for c in range(8):
                mm(pgb, WSD[sw][:, c, :], H3[:, c, t0:t0 + 512], c == 0, c == 7, r=[("wsd", sw), hr(c, ti)], w=["ps%d" % (6 + st)])
            for cc in range(8):
                mm(pyb, WHG[sw][:, cc, :], C3[:, cc, to:to + 512], cc == 0, cc == 7, r=[("whg", sw), ("C", cc, ti4)], w=["ps%d" % (4 + st)])
            act(T1, pga, AF.Sigmoid, r=["ps%d" % (2 + st)], w=[("tz", st, 0)])
            act(T2, pgb, AF.Sigmoid, r=["ps%d" % (6 + st)], w=[("tz", st, 1)])
            tt("dve", T1, pya, T1, ALU.mult, r=["ps%d" % st, ("tz", st, 0)], w=[("tz", st, 0)])
            stt(T2, pyb, PWB[:, o:o + 1], T2, ALU.add, ALU.mult, r=["ps%d" % (4 + st), ("tz", st, 1), "SM"], w=[("tz", st, 1)])
            tt("pool", B3[:, o, to:to + 512], T1, T2, ALU.add, r=[("tz", st, 0), ("tz", st, 1)], w=[("B", o, ti4)])
    P.barrier()

    for c in range(8):
        dma("sp", X3[:, c, :], xsp_d[:, c * TT:(c + 1) * TT], r=[("xsp", c)], w=[xr(c, ti) for ti in range(5)], key=("xrl", c))
    WS2 = [bfv(R_OFF + 4096 + i * 512, 512).rearrange("p (c f) -> p c f", c=8) for i in range(2)]
    for o in range(8):
        sw = o % 2
        dma("pool", WS2[sw], wout_d[o], r=[], w=[("ws2", sw)], key=("ws2", sw))
        for ti4 in range(4):
            st = (o * 4 + ti4) % 4
            ti = ti4 + 1
            t0 = HALO + 512 * ti4
            to = 512 * ti4
            px = PS[st]
            for m in range(8):
                mm(px, WS2[sw][:, m, :], B3[:, m, to:to + 512], m == 0, m == 7, r=[("ws2", sw), ("B", m, ti4)], w=["ps%d" % st])
            tt("dve", X3[:, o, t0:t0 + 512], px, X3[:, o, t0:t0 + 512], ALU.add, r=["ps%d" % st, xr(o, ti)], w=[xr(o, ti)])
    P.barrier()

    if DEBUG_STAGE == "mix":
        dump_X()
        P.op("sp", None, r=[("y", c) for c in range(8)], w=[])
        return nc, P, (arena_cm, ps_cms)

    ffn(1, TILES_OWN, OWN_IDS, G3)
    P.barrier()
    if DEBUG_STAGE == "ffn2":
        dump_X()
        P.op("sp", None, r=[("y", c) for c in range(8)], w=[])
        return nc, P, (arena_cm, ps_cms)
    for k, ((t0, w), ti) in enumerate(zip(TILES_OWN, OWN_IDS)):
        pn = PS[7]
        for c in range(8):
            sq = FM["sq"][c % 4]
            act(sq, X3[:, c, t0:t0 + w], AF.Square, r=[xr(c, ti)], w=[("sq", c % 4)])
            mm(pn, ONES, sq, c == 0, c == 7, r=["ONES", ("sq", c % 4)], w=["ps7"])
        rstd_act(RSTD[:, t0:t0 + w], pn, 1.0 / D, r=["ps7"], w=[("rstd", ti)])
        ost = FM["ost"][k % 2]
        for c in range(8):
            stt(ost[:, c, :], X3[:, c, t0:t0 + w], GF[:, c:c + 1], RSTD[:, t0:t0 + w], ALU.mult, ALU.mult,
                r=[xr(c, ti), ("rstd", ti), "SM"], w=[("ost", k % 2)])
        dma("sp", y_d[:, :, (t0 - HALO):(t0 - HALO) + 512], ost, r=[("ost", k % 2)], w=[("y", k)], key="out")
    P.op("sp", None, r=[("y", k) for k in range(4)], w=[])
    return nc, P, (arena_cm, ps_cms)


def emit_program(nc, P):
    keys = []
    for o in P.ops:
        if o.key is not None and o.key not in keys:
            keys.append(o.key)
    sem_cms = []

    def newsem(name):
        cm = nc.semaphore(name)
        sem_cms.append(cm)
        return cm.__enter__()

    esems = {e: newsem("se_" + e) for e in Prog.ENGS}
    dsems = {k: newsem("sd_%d" % i) for i, k in enumerate(keys)}
    streams = P.emit(nc, None, esems, dsems)
    with nc.Block() as block:
        @block.tensor
        def _(e):
            run_stream(streams["pe"], e, esems, dsems)

        @block.scalar
        def _(e):
            run_stream(streams["act"], e, esems, dsems)

        @block.vector
        def _(e):
            run_stream(streams["dve"], e, esems, dsems)

        @block.gpsimd
        def _(e):
            run_stream(streams["pool"], e, esems, dsems)

        @block.sync
        def _(e):
            run_stream(streams["sp"], e, esems, dsems)
    return nc


def _pm(w, nchunk):
    return np.ascontiguousarray(w.reshape(nchunk, 128, -1).transpose(1, 0, 2))


def _colblocks(w_pm, starts, width):
    return np.ascontiguousarray(np.concatenate([w_pm[:, :, s:s + width] for s in starts], axis=2))


def kernel(x, ffn1_norm, ffn1_w_gate, ffn1_w_up, ffn1_w_down, mix_norm, w_in, hgrn_lb_logits, hgrn_head_norm,
           hgrn_w_o, conv_w, conv_b, conv_ln_g, conv_ln_b, conv_w_pw, conv_b_pw, w_out, ffn2_norm, ffn2_w_gate,
           ffn2_w_up, ffn2_w_down, final_norm):
    f = lambda a: np.asarray(a, dtype=np.float32)
    x = f(x)
    win = _pm(f(w_in)[0], 8)
    shared = {
        "wg1": _pm(f(ffn1_w_gate)[0], 8), "wu1": _pm(f(ffn1_w_up)[0], 8), "wd1": _pm(f(ffn1_w_down)[0], NFC),
        "wg2": _pm(f(ffn2_w_gate)[0], 8), "wu2": _pm(f(ffn2_w_up)[0], 8), "wd2": _pm(f(ffn2_w_down)[0], NFC),
        "whg": np.stack([_colblocks(win, [h * 128, 1024 + h * 128, 2048 + h * 128, 3072 + h * 128], 128) for h in range(8)]),
        "wcv": np.stack([_colblocks(win, [4096 + c * 128, 5120 + c * 128], 128) for c in range(8)]),
        "wgab": np.stack([_colblocks(win, [6144 + c * 128, 7168 + c * 128], 128) for c in range(8)]),
    }

    def oblocks(w):
        wp = _pm(f(w)[0], 8)
        return np.stack([np.ascontiguousarray(wp[:, :, o * 128:(o + 1) * 128]) for o in range(8)])

    shared["wo"] = oblocks(hgrn_w_o)
    shared["wpw"] = oblocks(conv_w_pw)
    shared["wout"] = oblocks(w_out)

    def col8(v):
        return f(v).reshape(8, 128).T

    NSM = 8 * 13 + 2 + 8 * 31 + 2 + 24
    small = np.zeros((128, NSM), np.float32)
    small[:, 0:8] = col8(ffn1_norm[0])
    small[:, 8:16] = col8(mix_norm[0])
    small[:, 16:24] = col8(ffn2_norm[0])
    small[:, 24:32] = col8(final_norm)
    lbl = f(hgrn_lb_logits)
    small[:, 32:40] = col8(lbl[0])
    small[:, 40:48] = col8(lbl[1])
    small[:, 48:56] = col8(conv_b[0])
    small[:, 56:64] = col8(conv_ln_g[0])
    small[:, 64:72] = col8(conv_ln_b[0])
    small[:, 72:80] = col8(conv_b_pw[0])
    small[:, 96] = f(hgrn_head_norm)[0]
    small[:, 97] = EPS
    small[:, 354] = 1.0
    cw = f(conv_w)[0]
    small[:, 106:106 + 248] = cw.T.reshape(8, 128, 31).transpose(1, 0, 2).reshape(128, 248)
    shared["small"] = small

    cst = np.zeros((128, 128 * 2 + 512 * 3), np.float32)
    cst[:, 0:128] = np.eye(128, dtype=np.float32)
    cst[:, 128:256] = 1.0
    s_idx = np.arange(128)[:, None]
    t_idx = np.arange(128)[None, :]
    cm = ((s_idx // 64 == t_idx // 64) & (s_idx <= t_idx)).astype(np.float32)
    cst[:, 256:768] = np.tile(cm, (1, 4))
    rs = np.ones(512, np.float32)
    rs[0::64] = 0.0
    cst[:, 768:1280] = rs[None, :]
    cst[:, 1280:1792] = 1.0
    shared["cst"] = cst

    in_maps = []
    for c in range(NCORES):
        b, j = c // 4, c % 4
        own = x[b, j * T:(j + 1) * T, :]
        halo = x[b, j * T - HALO:j * T, :] if j > 0 else np.zeros((HALO, D), np.float32)
        xc = np.concatenate([halo, own], axis=0).T
        xc = xc.reshape(8, 128, TT).transpose(1, 0, 2).reshape(128, 8 * TT)
        inc = np.zeros((128, 16), np.float32)
        for r_ in range(4):
            v = 1.0 if r_ < j else 0.0
            inc[:, r_] = v
            inc[:, 8 + r_] = 1.0 - v
        m = dict(shared)
        m["xT"] = np.ascontiguousarray(xc)
        m["inc"] = inc
        in_maps.append(m)

    nc, P, keep = build_program()
    emit_program(nc, P)
    res = run_bass_kernel_spmd(nc, in_maps, core_ids=list(range(NCORES)))
    out = np.empty((2, 4 * T, D), np.float32)
    for c in range(NCORES):
        b, j = c // 4, c % 4
        yc = np.asarray(res.results[c]["y"]).reshape(128, 8, T)
        out[b, j * T:(j + 1) * T, :] = yc.transpose(2, 1, 0).reshape(T, D)
    return out
```

```python
import bisect
import numpy as np
import concourse.bass as bass
import concourse.mybir as mybir
from concourse.bass_utils import run_bass_kernel_spmd

F32 = mybir.dt.float32
BF16 = mybir.dt.bfloat16
AF = mybir.ActivationFunctionType
ALU = mybir.AluOpType

NCORES = 8
D = 1024
DFF = 2816
NFC = 22
T = 2048
HALO = 32
TT = T + HALO
EPS = 1e-6
TILES_ALL = [(0, HALO)] + [(HALO + 512 * i, 512) for i in range(4)]
TILES_OWN = TILES_ALL[1:]
FGROUPS = [(0, 4), (4, 4), (8, 4), (12, 4), (16, 4), (20, 2)]

DEBUG_STAGE = "full"
NO_CC = False


class Op:
    __slots__ = ("eng", "fn", "deps", "signal", "ord", "key", "gidx", "unit")

    def __init__(self, eng, fn, key=None, unit=16):
        self.eng = eng
        self.fn = fn
        self.deps = set()
        self.signal = False
        self.ord = 0
        self.key = key
        self.gidx = 0
        self.unit = unit


class Prog:
    ENGS = ("pe", "act", "dve", "pool", "sp")

    def __init__(self):
        self.ops = []
        self.wlast = {}
        self.rlast = {}
        self.pending_bar = {}
        self.last_on_eng = {}
        self.last_dma = {}

    def op(self, eng, fn, r=(), w=(), key=None, unit=16):
        o = Op(eng, fn, key, unit)
        o.gidx = len(self.ops)
        is_dma = key is not None
        deps = set()
        for res in r:
            for p in self.wlast.get(res, {}).values():
                deps.add((p, "raw"))
        for res in w:
            for p in self.wlast.get(res, {}).values():
                deps.add((p, "waw"))
            for p in self.rlast.get(res, {}).values():
                deps.add((p, "war"))
        for (p, kind) in deps:
            if p is o:
                continue
            if p.eng == eng and p.key is None and not is_dma:
                if eng == "pe":
                    continue
            o.deps.add(p)
        if eng in self.pending_bar:
            o.deps |= self.pending_bar.pop(eng)
        best = {}
        keep = set()
        for p in o.deps:
            if p.key is None:
                if p.eng not in best or p.gidx > best[p.eng].gidx:
                    best[p.eng] = p
            else:
                keep.add(p)
        o.deps = keep | set(best.values())
        th = o.eng if key is None else ("dma", key)
        for res in r:
            self.rlast.setdefault(res, {})[th] = o
        for res in w:
            self.wlast.setdefault(res, {})[th] = o
        self.ops.append(o)
        if is_dma:
            self.last_dma[key] = o
        else:
            self.last_on_eng[eng] = o
        return o

    def barrier(self):
        allp = set(self.last_on_eng.values()) | set(self.last_dma.values())
        for e in self.ENGS:
            self.pending_bar[e] = set(allp) | self.pending_bar.get(e, set())

    def emit(self, nc, ehandles, esems, dsems):
        for o in self.ops:
            for p in o.deps:
                if p.key is None:
                    p.signal = True
        cnt = {e: 0 for e in self.ENGS}
        keyidx = {}
        for o in self.ops:
            if o.key is None:
                if o.signal:
                    cnt[o.eng] += 1
                o.ord = cnt[o.eng]
            else:
                keyidx.setdefault(o.key, []).append(o.gidx)
        waited = {}
        streams = {e: [] for e in self.ENGS}
        for o in self.ops:
            waits = {}
            for p in o.deps:
                if p.key is None:
                    sem = ("e", p.eng)
                    val = p.ord
                else:
                    sem = ("d", p.key)
                    val = p.unit * bisect.bisect_left(keyidx[p.key], o.gidx)
                if val > waits.get(sem, 0):
                    waits[sem] = val
            wl = []
            for sem, val in waits.items():
                if waited.get((o.eng, sem), 0) >= val:
                    continue
                waited[(o.eng, sem)] = val
                wl.append((sem, val))
            streams[o.eng].append((o, wl))
        return streams


def run_stream(stream, eh, esems, dsems):
    for (o, wl) in stream:
        for (sem, val) in wl:
            s = esems[sem[1]] if sem[0] == "e" else dsems[sem[1]]
            eh.wait_ge(s, val)
        if o.fn is None:
            continue
        ins = o.fn(eh)
        if o.key is not None:
            if o.unit == 16:
                ins.then_inc(dsems[o.key], 16)
            else:
                ins.then_inc(dsems[o.key], 1)
        elif o.signal:
            ins.then_inc(esems[o.eng], 1)


def build_program():
    nc = bass.Bass("TRN2", target_bir_lowering=False)

    def din(name, shape):
        return nc.dram_tensor(name, list(shape), F32, kind="ExternalInput").ap()

    xT = din("xT", [128, 8 * TT])
    wg_d = [din("wg1", [128, 8, DFF]), din("wg2", [128, 8, DFF])]
    wu_d = [din("wu1", [128, 8, DFF]), din("wu2", [128, 8, DFF])]
    wd_d = [din("wd1", [128, NFC, D]), din("wd2", [128, NFC, D])]
    whg_d = din("whg", [8, 128, 8, 512])
    wcv_d = din("wcv", [8, 128, 8, 256])
    wgab_d = din("wgab", [8, 128, 8, 256])
    wo_d = din("wo", [8, 128, 8, 128])
    wpw_d = din("wpw", [8, 128, 8, 128])
    wout_d = din("wout", [8, 128, 8, 128])
    NSM = 8 * 13 + 2 + 8 * 31 + 2 + 24
    small_d = din("small", [128, NSM])
    cst_d = din("cst", [128, 128 * 2 + 512 * 3])
    inc_d = din("inc", [128, 16])
    y_d = nc.dram_tensor("y", [128, 8, T], F32, kind="ExternalOutput").ap()
    xsp_d = nc.dram_tensor("xsp", [128, 8 * TT], F32, kind="Internal").ap()
    GW = 8 * 128 + 8
    gsrc_d = nc.dram_tensor("gsrc", [128, GW], F32, kind="Internal").ap()
    gdst_d = nc.dram_tensor("gdst", [4 * 128, GW], F32, kind="Internal").ap()

    NW = 52000
    P = Prog()

    arena_cm = nc.sbuf_tensor("arena", [128, NW], F32)
    arena_t = arena_cm.__enter__()
    AR = arena_t[:]
    ps_cms = [nc.psum_tensor("ps%d" % i, [128, 512], F32) for i in range(8)]
    PS = [cm.__enter__()[:] for cm in ps_cms]

    def f32v(off, n):
        return AR[:, off:off + n]

    def bfv(off, nwords):
        return AR[:, off:off + nwords].bitcast(BF16)

    X_OFF = 0
    XW = 8 * TT
    H_OFF = X_OFF + XW
    HW = 8 * TT // 2
    C_OFF = H_OFF + HW
    P_OFF = C_OFF + 4224
    X3 = f32v(X_OFF, XW).rearrange("p (c t) -> p c t", c=8)
    H3 = bfv(H_OFF, HW).rearrange("p (c t) -> p c t", c=8)

    co = [C_OFF]

    def calloc(n):
        o = co[0]
        co[0] += n
        assert co[0] <= P_OFF
        return o

    SM = f32v(calloc(NSM), NSM)
    G1, G2, G3, GF = SM[:, 0:8], SM[:, 8:16], SM[:, 16:24], SM[:, 24:32]
    LB0, LB1 = SM[:, 32:40], SM[:, 40:48]
    CONVB, LNG, LNB, PWB = SM[:, 48:56], SM[:, 56:64], SM[:, 64:72], SM[:, 72:80]
    LB, OML = SM[:, 80:88], SM[:, 88:96]
    HN = SM[:, 96:97]
    LBD = SM[:, 98:106]
    CONVW = SM[:, 106:106 + 248].rearrange("p (c k) -> p c k", c=8)
    INC = f32v(calloc(16), 16)
    IDENT = bfv(calloc(64), 64)
    ONES = bfv(calloc(64), 64)
    CMASK4 = f32v(calloc(512), 512)
    RESET = f32v(calloc(512), 512)
    ONES32 = f32v(calloc(512), 512)
    RSTD = f32v(calloc(TT), TT)
    BTOT = f32v(calloc(8), 8)

    def ffn_map():
        o = P_OFF
        m = {}
        m["wg"] = [bfv(o, 2048).rearrange("p (c f) -> p c f", c=8), bfv(o + 2048, 2048).rearrange("p (c f) -> p c f", c=8)]
        o += 4096
        m["wu"] = [bfv(o, 2048).rearrange("p (c f) -> p c f", c=8), bfv(o + 2048, 2048).rearrange("p (c f) -> p c f", c=8)]
        o += 4096
        m["wd"] = [bfv(o, 2048).rearrange("p (j d) -> p j d", j=4), bfv(o + 2048, 2048).rearrange("p (j d) -> p j d", j=4)]
        o += 4096
        m["a"] = [[bfv(o + (ab * 4 + j) * 256, 256) for j in range(4)] for ab in range(2)]
        o += 2048
        m["sl"] = [f32v(o, 512), f32v(o + 512, 512)]
        o += 1024
        m["sq"] = [bfv(o + i * 256, 256) for i in range(4)]
        o += 1024
        assert o <= NW
        m["ost"] = [f32v(P_OFF, 4096).rearrange("p (c t) -> p c t", c=8), f32v(P_OFF + 4096, 4096).rearrange("p (c t) -> p c t", c=8)]
        return m

    FM = ffn_map()

    def dma(eng, out, in_, r, w, key):
        P.op(eng, lambda e, out=out, in_=in_: e.dma_start(out=out, in_=in_), r=r, w=w, key=key)

    def mm(out, lhsT, rhs, start, stop, r, w):
        P.op("pe", lambda e, out=out, lhsT=lhsT, rhs=rhs, start=start, stop=stop:
             e.matmul(out, lhsT, rhs, start=start, stop=stop), r=r, w=w)

    def act(out, in_, func, r, w, bias=None, scale=None):
        def fn(e, out=out, in_=in_, func=func, bias=bias, scale=scale):
            kw = {}
            if bias is not None:
                kw["bias"] = bias
            if scale is not None:
                kw["scale"] = scale
            return e.activation(out, in_, func, **kw)
        P.op("act", fn, r=r, w=w)

    def tt(eng, out, in0, in1, op, r, w):
        P.op(eng, lambda e, out=out, in0=in0, in1=in1, op=op: e.tensor_tensor(out, in0, in1, op), r=r, w=w)

    def ts(eng, out, in0, s1, s2, op0, op1, r, w):
        if op1 is None:
            P.op(eng, lambda e, out=out, in0=in0, s1=s1, op0=op0: e.tensor_scalar(out, in0, s1, None, op0), r=r, w=w)
        else:
            P.op(eng, lambda e, out=out, in0=in0, s1=s1, s2=s2, op0=op0, op1=op1:
                 e.tensor_scalar(out, in0, s1, s2, op0, op1), r=r, w=w)

    def stt(out, in0, scalar, in1, op0, op1, r, w):
        P.op("dve", lambda e, out=out, in0=in0, scalar=scalar, in1=in1, op0=op0, op1=op1:
             e.scalar_tensor_tensor(out, in0, scalar, in1, op0, op1), r=r, w=w)

    def cp(eng, out, in_, r, w):
        if eng == "act":
            act(out, in_, AF.Copy, r, w)
        else:
            P.op(eng, lambda e, out=out, in_=in_: e.tensor_copy(out, in_), r=r, w=w)

    def xr(c, ti):
        return ("X", c, ti)

    def hr(c, ti):
        return ("H", c, ti)

    dma("sp", SM, small_d, r=[], w=["SM"], key="cst")
    dma("sp", INC, inc_d, r=[], w=["INC"], key="cst")
    dma("pool", ONES, cst_d[:, 128:256], r=[], w=["ONES"], key="cstb")
    xT3 = xT.rearrange("p (c t) -> p c t", c=8)
    def load_x_tile(ti, extra_r=()):
        t0, w = TILES_ALL[ti]
        dma("sp", X3[:, :, t0:t0 + w], xT3[:, :, t0:t0 + w], r=list(extra_r), w=[xr(c, ti) for c in range(8)], key=("xin", ti))

    load_x_tile(0)
    load_x_tile(1)
    DEFER_X = [True]
    dma("pool", IDENT, cst_d[:, 0:128], r=[], w=["IDENT"], key="cstb2")
    dma("sp", CMASK4, cst_d[:, 256:768], r=[], w=["CMASK"], key="cst2")
    dma("sp", RESET, cst_d[:, 768:1280], r=[], w=["RESET"], key="cst2")
    dma("sp", ONES32, cst_d[:, 1280:1792], r=[], w=["ONES32"], key="cst2")
    tt("dve", LBD, LB0, LB1, ALU.subtract, r=["SM"], w=["LBD"])
    act(LB, LBD, AF.Sigmoid, r=["LBD"], w=["LB"])
    ts("dve", OML, LB, -1.0, 1.0, ALU.mult, ALU.add, r=["LB"], w=["OML"])
    SMIN = SM[:, 356:364]
    NOML = SM[:, 364:372]
    ROML = SM[:, 372:380]
    ts("dve", NOML, OML, -1.0, None, ALU.mult, None, r=["OML"], w=["NOML"])
    P.op("dve", lambda e: e.reciprocal(ROML, OML), r=["OML"], w=["ROML"])
    ts("dve", SMIN, LB, -1.0, 1e-6, ALU.mult, ALU.add, r=["LB"], w=["SMIN0"])
    tt("dve", SMIN, SMIN, ROML, ALU.mult, r=["SMIN0", "ROML"], w=["SMIN"])

    def rms_norm(gcols, tiles, tile_ids):
        for (t0, w), ti in zip(tiles, tile_ids):
            norm_tile(gcols, t0, w, ti)

    def norm_tile(gcols, t0, w, ti):
        if True:
            pn = PS[7]
            for c in range(8):
                sq = FM["sq"][c % 4]
                act(sq[:, :w], X3[:, c, t0:t0 + w], AF.Square, r=[xr(c, ti)], w=[("sq", c % 4)])
                mm(pn[:, :w], ONES, sq[:, :w], c == 0, c == 7, r=["ONES", ("sq", c % 4)], w=["ps7"])
            rstd_act(RSTD[:, t0:t0 + w], pn[:, :w], 1.0 / D, r=["ps7"], w=[("rstd", ti)])
            for c in range(8):
                stt(H3[:, c, t0:t0 + w], X3[:, c, t0:t0 + w], gcols[:, c:c + 1], RSTD[:, t0:t0 + w],
                    ALU.mult, ALU.mult, r=[xr(c, ti), ("rstd", ti), "SM"], w=[hr(c, ti)])

    EPSB = SM[:, 97:98]
    ONEB = SM[:, 354:355]

    def rstd_act(out, in_, scale, r, w):
        act(out, in_, AF.Ln, r=r, w=w, bias=EPSB, scale=scale)
        act(out, out, AF.Exp, r=w, w=w, scale=-0.5)

    def sigmoid_act(out, in_, r, w):
        act(out, in_, AF.Exp, r=r, w=w, scale=-1.0)
        act(out, out, AF.Ln, r=w, w=w, bias=ONEB, scale=1.0)
        act(out, out, AF.Exp, r=w, w=w, scale=-1.0)

    PRELOADED = set()
    WOVR = {}

    def ffn_load_gu(which, gi, extra_r=()):
        j0, nf = FGROUPS[gi]
        s = gi % 2
        dma("pool", FM["wg"][s][:, :, :nf * 128], wg_d[which][:, :, j0 * 128:(j0 + nf) * 128], r=list(extra_r), w=[("wg", s)], key=("wg", s))
        dma("pool", FM["wu"][s][:, :, :nf * 128], wu_d[which][:, :, j0 * 128:(j0 + nf) * 128], r=list(extra_r), w=[("wu", s)], key=("wu", s))

    def ffn(which, tiles, tile_ids, gcols):
        steps = [(gi, k) for gi in range(len(FGROUPS)) for k in range(len(tiles))]

        def load(gi):
            j0, nf = FGROUPS[gi]
            s = gi % 2
            if (which, gi) not in PRELOADED and (which, gi) not in WOVR:
                ffn_load_gu(which, gi)
            dma("pool", FM["wd"][s][:, :nf, :], wd_d[which][:, j0:j0 + nf, :], r=[], w=[("wd", s)], key=("wd", s))

        def gu(si):
            gi, k = steps[si]
            j0, nf = FGROUPS[gi]
            s = gi % 2
            ab = si % 2
            (t0, w), ti = tiles[k], tile_ids[k]
            if (which, gi) in WOVR:
                WG, WU, rg, ru = WOVR[(which, gi)]
            else:
                WG, WU, rg, ru = FM["wg"][s], FM["wu"][s], ("wg", s), ("wu", s)
            for j in range(nf):
                pg, pu = PS[j % 2], PS[2 + j % 2]
                for c in range(8):
                    mm(pg[:, :w], WG[:, c, j * 128:(j + 1) * 128], H3[:, c, t0:t0 + w], c == 0, c == 7,
                       r=[rg, hr(c, ti)], w=["ps%d" % (j % 2)])
                for c in range(8):
                    mm(pu[:, :w], WU[:, c, j * 128:(j + 1) * 128], H3[:, c, t0:t0 + w], c == 0, c == 7,
                       r=[ru, hr(c, ti)], w=["ps%d" % (2 + j % 2)])
                sl = FM["sl"][j % 2]
                act(sl[:, :w], pg[:, :w], AF.Silu, r=["ps%d" % (j % 2)], w=[("sl", j % 2)])
                tt("dve", FM["a"][ab][j][:, :w], pu[:, :w], sl[:, :w], ALU.mult,
                   r=["ps%d" % (2 + j % 2), ("sl", j % 2)], w=[("a", ab, j)])

        def dn(si):
            gi, k = steps[si]
            j0, nf = FGROUPS[gi]
            s = gi % 2
            ab = si % 2
            (t0, w), ti = tiles[k], tile_ids[k]
            for o in range(8):
                pd = PS[4 + o % 4]
                for j in range(nf):
                    mm(pd[:, :w], FM["wd"][s][:, j, o * 128:(o + 1) * 128], FM["a"][ab][j][:, :w], j == 0, j == nf - 1,
                       r=[("wd", s), ("a", ab, j)], w=["ps%d" % (4 + o % 4)])
                stt(X3[:, o, t0:t0 + w], pd[:, :w], 0.5, X3[:, o, t0:t0 + w], ALU.mult, ALU.add,
                    r=["ps%d" % (4 + o % 4), xr(o, ti)], w=[xr(o, ti)])

        load(0)
        if which == 0 and DEFER_X[0]:
            DEFER_X[0] = False
            for ti_ in range(2, 5):
                load_x_tile(ti_, extra_r=[("wu", 0)])
        for si in range(len(steps)):
            if steps[si][0] == 0:
                k0 = steps[si][1]
                norm_tile(gcols, tiles[k0][0], tiles[k0][1], tile_ids[k0])
            gu(si)
            if si == 0 and len(FGROUPS) > 1:
                load(1)
            if si > 0:
                dn(si - 1)
                gi_prev, k_prev = steps[si - 1]
                if k_prev == len(tiles) - 1 and gi_prev + 2 < len(FGROUPS):
                    load(gi_prev + 2)
        dn(len(steps) - 1)

    ALL_IDS = list(range(5))
    OWN_IDS = list(range(1, 5))
    ffn(0, TILES_ALL, ALL_IDS, G1)

    out_cnt = [0]

    def dump_X():
        P.barrier()
        for c in range(8):
            dma("sp", y_d[:, c, :], X3[:, c, HALO:HALO + T], r=[xr(c, ti) for ti in range(5)], w=[("y", c)], key="out")
            out_cnt[0] += 1

    def finish():
        P.op("sp", None, r=[("y", c) for c in range(8)] + ["yfin"], w=[])
        P.barrier()

    if DEBUG_STAGE == "ffn1":
        dump_X()
        P.op("sp", None, r=[("y", c) for c in range(8)], w=[])
        return nc, P, (arena_cm, ps_cms)

    rms_norm(G2, TILES_ALL, ALL_IDS)
    for c in range(8):
        dma("sp", xsp_d[:, c * TT:(c + 1) * TT], X3[:, c, :], r=[xr(c, ti) for ti in range(5)], w=[("xsp", c)], key=("xsp", c))
    P.barrier()

    A_OFF = P_OFF
    B_OFF = P_OFF + 8192
    R_OFF = P_OFF + 16384
    A3 = bfv(A_OFF, 8192).rearrange("p (c t) -> p c t", c=8)
    B3 = bfv(B_OFF, 8192).rearrange("p (c t) -> p c t", c=8)

    xo = [X_OFF]

    def xalloc(n):
        o = xo[0]
        xo[0] += n
        assert xo[0] <= X_OFF + XW, xo[0]
        return o

    ro = [R_OFF]

    def ralloc(n):
        o = ro[0]
        ro[0] += n
        assert ro[0] <= NW, ro[0]
        return o

    WH = [bfv(xalloc(2048), 2048).rearrange("p (c f) -> p c f", c=8) for _ in range(2)]
    TF = [[f32v(xalloc(512), 512) for _ in range(5)] for _ in range(2)]
    TB = [[bfv(xalloc(256), 256) for _ in range(6)] for _ in range(3)]
    SALL = [f32v(xalloc(1024), 1024).rearrange("p (n d) -> p n d", n=8) for _ in range(2)]
    GS = f32v(ralloc(GW), GW)
    GS3 = GS[:, 0:1024].rearrange("p (h d) -> p h d", h=8)
    BT = f32v(ralloc(64), 64)
    BT3 = BT.rearrange("p (r h) -> p r h", r=8)
    NR = 4
    EB_ = f32v(ralloc(64), 64)
    CR = f32v(ralloc(64), 64)
    RS = f32v(ralloc(1024), 1024)
    RS3 = RS.rearrange("p (h d) -> p h d", h=8)
    GL = [f32v(ralloc(1024), 1024) for _ in range(2)]
    SIN = bfv(ralloc(512), 512).rearrange("p (h d) -> p h d", h=8)
    SBF = [bfv(ralloc(512), 512).rearrange("p (n d) -> p n d", n=8) for _ in range(2)]
    EBL = [f32v(ralloc(8), 8) for _ in range(3)]

    pq, pf, pv, ptr, pA, pU0, pU1, po = PS
    ptr_bf = ptr.bitcast(BF16)
    NT = 32

    def tinfo(k):
        hd, ti4 = k // 4, k % 4
        return hd, ti4, ti4 + 1, HALO + 512 * ti4, 512 * ti4

    def load_wh(hd):
        s_w = hd % 2
        dma("pool", WH[s_w][:, :, 0:384], whg_d[hd, :, :, 0:384], r=[], w=[("wh", s_w)], key=("wh", s_w))

    def F1(k):
        hd, ti4, ti, t0, to = tinfo(k)
        s_w, f, b = hd % 2, k % 2, k % 3
        if ti4 == 0 and hd + 1 < 8:
            load_wh(hd + 1)
        for c in range(8):
            mm(pq, WH[s_w][:, c, 0:128], H3[:, c, t0:t0 + 512], c == 0, c == 7, r=[("wh", s_w), hr(c, ti)], w=["ps0"])
        for s in range(4):
            for c in range(8):
                mm(pv[:, s * 128:(s + 1) * 128], H3[:, c, t0 + s * 128:t0 + (s + 1) * 128], WH[s_w][:, c, 256:384],
                   c == 0, c == 7, r=[("wh", s_w), hr(c, ti)], w=["ps2"])

    def F1f(k):
        hd, ti4, ti, t0, to = tinfo(k)
        s_w = hd % 2
        for c in range(8):
            mm(pf, WH[s_w][:, c, 128:256], H3[:, c, t0:t0 + 512], c == 0, c == 7, r=[("wh", s_w), hr(c, ti)], w=["ps1"])

    def S1(k):
        f, b = k % 2, k % 3
        T1 = TF[f][0]
        VTOK = TB[b][4]
        sigmoid_act(T1, pf, r=["ps1"], w=[("tf", f, 0)])

    def E0(k):
        hd, ti4, ti, t0, to = tinfo(k)
        f = k % 2
        T1, T2 = TF[f][0], TF[f][1]
        rT = [("tf", f, i) for i in range(5)]
        ts("dve", T1, T1, SMIN[:, hd:hd + 1], None, ALU.max, None, r=[rT[0], "SMIN"], w=[rT[0]])
        act(T2, T1, AF.Ln, r=[rT[0], "LB", "OML"], w=[rT[1]], bias=LB[:, hd:hd + 1], scale=OML[:, hd:hd + 1])

    def E(k):
        hd, ti4, ti, t0, to = tinfo(k)
        f, b = k % 2, k % 3
        fo = 1 - f
        T1, T2, T3, T4, T5 = TF[f]
        QE, KE, KHT, KHTOK, VTOK, AM = TB[b]
        rT = [("tf", f, i) for i in range(5)]
        rB = [("tb", b, i) for i in range(6)]
        P.op("dve", lambda e, o=T3, d1=T2: e.tensor_tensor_scan(o, RESET, d1, 0.0, ALU.mult, ALU.add),
             r=[rT[1], "RESET"], w=[rT[2]])
        if ti4 == 0:
            P.op("dve", lambda e, o=T4, d1=T2: e.tensor_tensor_scan(o, ONES32, d1, 0.0, ALU.mult, ALU.add),
                 r=[rT[1], "ONES32"], w=[rT[3]])
        else:
            prevT4 = TF[fo][3]
            P.op("dve", lambda e, o=T4, d1=T2, ini=prevT4[:, 511:512]: e.tensor_tensor_scan(o, ONES32, d1, ini, ALU.mult, ALU.add),
                 r=[rT[1], "ONES32", ("tf", fo, 3)], w=[rT[3]])
        ts("pool", T1, T1, NOML[:, hd:hd + 1], OML[:, hd:hd + 1], ALU.mult, ALU.add, r=[rT[0], "NOML", "OML"], w=[rT[0]])

    def Ec1(k):
        f = k % 2
        T1, T2, T3, T4, T5 = TF[f]
        rT = [("tf", f, i) for i in range(5)]
        act(T5, T3, AF.Exp, r=[rT[2]], w=[rT[4]])

    def Ec(k):
        f = k % 2
        T1, T2, T3, T4, T5 = TF[f]
        rT = [("tf", f, i) for i in range(5)]
        act(T2, T4, AF.Exp, r=[rT[3], rT[1]], w=[rT[1]])
        act(T3, T3, AF.Exp, r=[rT[2]], w=[rT[2]], scale=-1.0)

    def Eb(k):
        hd, ti4, ti, t0, to = tinfo(k)
        f, b = k % 2, k % 3
        T1, T2, T3, T4, T5 = TF[f]
        QE, KE, KHT, KHTOK, VTOK, AM = TB[b]
        rT = [("tf", f, i) for i in range(5)]
        rB = [("tb", b, i) for i in range(6)]
        cp("act", VTOK, pv, r=["ps2"], w=[("tb", b, 4)])
        tt("dve", QE, pq, T5, ALU.mult, r=["ps0", rT[4]], w=[rB[0]])
        tt("dve", B3[:, hd, to:to + 512], pq, T2, ALU.mult, r=["ps0", rT[1]], w=[("B", hd, ti4)])
        tt("pool", KE, T1, T3, ALU.mult, r=[rT[0], rT[2]], w=[rB[1]])
        tt("pool", KHT.rearrange("p (n t) -> p n t", n=8), KE.rearrange("p (n t) -> p n t", n=8),
           T5[:, 63:512:64].to_broadcast([128, 8, 64]), ALU.mult, r=[rB[1], rT[4]], w=[rB[2]])
        cp("pool", EBL[b], T5[:, 63:512:64], r=[rT[4]], w=[("ebl", b)])
        if ti4 == 3:
            cp("pool", GS[:, 1024 + hd:1025 + hd], T4[:, 511:512], r=[rT[3]], w=["GS"])

    def F2(k):
        b = k % 3
        QE, KE, KHT, KHTOK, VTOK, AM = TB[b]
        rB = [("tb", b, i) for i in range(6)]
        for s in range(4):
            P.op("pe", lambda e, o=ptr_bf[:, s * 128:(s + 1) * 128], i=KHT[:, s * 128:(s + 1) * 128]: e.transpose(o, i, IDENT),
                 r=[rB[2], "IDENT"], w=["ps3"])
        cp("act", KHTOK, ptr_bf[:, 0:512], r=["ps3"], w=[rB[3]])
        for s in range(4):
            mm(pA[:, s * 128:(s + 1) * 128], KE[:, s * 128:(s + 1) * 128], QE[:, s * 128:(s + 1) * 128], True, True,
               r=[rB[0], rB[1]], w=["ps4"])

    def F2m(k):
        b = k % 3
        AM = TB[b][5]
        tt("dve", AM, pA, CMASK4, ALU.mult, r=["ps4", "CMASK"], w=[("tb", b, 5)])

    def F2u(k):
        b = k % 3
        QE, KE, KHT, KHTOK, VTOK, AM = TB[b]
        rB = [("tb", b, i) for i in range(6)]
        for n in range(8):
            s, half = n // 2, n % 2
            r0 = half * 64
            pu = pU0 if half == 0 else pU1
            mm(pu[:, s * 128:(s + 1) * 128], KHTOK[r0:r0 + 64, s * 128:(s + 1) * 128],
               VTOK[r0:r0 + 64, s * 128:(s + 1) * 128], True, True, r=[rB[3], rB[4]], w=["ps%d" % (5 + half)])

    def Bk(k):
        hd, ti4, ti, t0, to = tinfo(k)
        b, st = k % 3, k % 2
        so = 1 - st
        QE, KE, KHT, KHTOK, VTOK, AM = TB[b]
        rB = [("tb", b, i) for i in range(6)]
        for n in range(8):
            pu = pU0 if n % 2 == 0 else pU1
            pun = pu[:, (n // 2) * 128:(n // 2 + 1) * 128]
            if n == 0 and ti4 == 0:
                cp("dve", SALL[st][:, 0, :], pun, r=["ps5"], w=[("sall", st)])
            else:
                prev = SALL[st][:, n - 1, :] if n > 0 else SALL[so][:, 7, :]
                rr = [("sall", st), "ps%d" % (5 + n % 2), ("ebl", b)] + ([("sall", so)] if n == 0 else [])
                stt(SALL[st][:, n, :], prev, EBL[b][:, n:n + 1], pun, ALU.mult, ALU.add, r=rr, w=[("sall", st)])
        if ti4 == 0:
            P.op("pool", lambda e, o=SBF[st][:, 0, :]: e.memset(o, 0.0), r=[], w=[("sbf", st)])
        else:
            cp("pool", SBF[st][:, 0, :], SALL[so][:, 7, :], r=[("sall", so)], w=[("sbf", st)])
        if ti4 == 3:
            cp("pool", GS3[:, hd, :], SALL[st][:, 7, :], r=[("sall", st)], w=["GS"])

    def Bcast(k):
        st = k % 2
        cp("act", SBF[st][:, 1:8, :], SALL[st][:, 0:7, :], r=[("sall", st)], w=[("sbf", st)])

    def Bo(k):
        hd, ti4, ti, t0, to = tinfo(k)
        b, st = k % 3, k % 2
        QE, KE, KHT, KHTOK, VTOK, AM = TB[b]
        rB = [("tb", b, i) for i in range(6)]
        for s in range(4):
            mm(po[:, s * 128:(s + 1) * 128], VTOK[:, s * 128:(s + 1) * 128], AM[:, s * 128:(s + 1) * 128], True, False,
               r=[rB[4], rB[5]], w=["ps7"])
            mm(po[:, s * 128:s * 128 + 64], SBF[st][:, 2 * s, :], QE[:, s * 128:s * 128 + 64], False, False,
               r=[("sbf", st), rB[0]], w=["ps7"])
            mm(po[:, s * 128 + 64:s * 128 + 128], SBF[st][:, 2 * s + 1, :], QE[:, s * 128 + 64:s * 128 + 128], False, True,
               r=[("sbf", st), rB[0]], w=["ps7"])
        cp("act", A3[:, hd, to:to + 512], po, r=["ps7"], w=[("A", hd, ti4)])

    load_wh(0)
    F1f(0)
    for step in range(NT + 2):
        if 0 <= step - 1 < NT:
            F2(step - 1)
        if step < NT:
            F1(step)
            S1(step)
        if 0 <= step - 2 < NT:
            Bk(step - 2)
        if 0 <= step - 1 < NT:
            F2m(step - 1)
        if step < NT:
            E0(step)
        if 0 <= step - 2 < NT:
            Bcast(step - 2)
        if step < NT:
            E(step)
            Ec1(step)
        if 0 <= step - 2 < NT:
            Bo(step - 2)
        if 0 <= step - 1 < NT:
            F2u(step - 1)
        if step < NT:
            Ec(step)
            Eb(step)
        if step + 1 < NT:
            F1f(step + 1)

    P.barrier()
    dma("sp", gsrc_d, GS, r=["GS"], w=["gsrc"], key="gs")
    if NO_CC:
        dma("sp", gdst_d[0:128, :], gsrc_d, r=["gsrc"], w=["gdst"], key="cc")
    else:
        P.op("pool", lambda e: e.collective_compute("AllGather", ALU.bypass, replica_groups=[[0, 1, 2, 3], [4, 5, 6, 7]],
                                                    ins=[gsrc_d], outs=[gdst_d]),
             r=["gsrc"], w=["gdst"], key="cc", unit=1)

    CW_OFF = X_OFF
    WSC = [bfv(CW_OFF + i * 1024, 1024).rearrange("p (c f) -> p c f", c=8) for i in range(2)]
    DG = [bfv(CW_OFF + 2048 + 64 * k, 64) for k in range(31)]
    UB = [bfv(CW_OFF + 4032 + i * (TT // 2), TT // 2) for i in range(2)]
    TC = [f32v(CW_OFF + 6112 + i * 512, 512) for i in range(2)]
    NDV = 3
    C_OFF3 = CW_OFF + 7136
    assert C_OFF3 + 8192 <= X_OFF + XW
    C3 = bfv(C_OFF3, 8192).rearrange("p (c t) -> p c t", c=8)
    def emit_combine():
        gd3 = gdst_d.rearrange("(r p) c -> p r c", p=128)
        for r_ in range(NR):
            dma("sp", BT3[:, r_, :], gd3[:, r_, 1024:1032], r=["gdst"], w=["BT"], key="bt")
        act(EB_[:, 0:8 * NR], BT[:, 0:8 * NR], AF.Exp, r=["BT"], w=["EB_"])
        P.op("dve", lambda e: e.memset(RS, 0.0), r=[], w=["RS"])
        for r_ in range(NR):
            gl = GL[r_ % 2]
            dma("sp", gl, gd3[:, r_, 0:1024], r=["gdst"], w=[("gl", r_ % 2)], key=("gl", r_ % 2))
            ts("dve", CR[:, r_ * 8:(r_ + 1) * 8], EB_[:, r_ * 8:(r_ + 1) * 8], INC[:, r_:r_ + 1], INC[:, 8 + r_:9 + r_],
               ALU.mult, ALU.add, r=["EB_", "INC"], w=[("cr", r_)])
            ts("dve", gl, gl, INC[:, r_:r_ + 1], None, ALU.mult, None, r=[("gl", r_ % 2), "INC"], w=[("gl", r_ % 2)])
            for h in range(8):
                stt(RS3[:, h, :], RS3[:, h, :], CR[:, r_ * 8 + h:r_ * 8 + h + 1], gl[:, h * 128:(h + 1) * 128], ALU.mult, ALU.add,
                    r=["RS", ("cr", r_), ("gl", r_ % 2)], w=["RS"])
        cp("act", SIN, RS3, r=["RS"], w=["SIN"])


    for cc in range(8):
        sw = cc % 2
        if cc == 5:
            emit_combine()
        dma("pool", WSC[sw], wcv_d[cc], r=[], w=[("wsc", sw)], key=("wsc", sw))
        for k in range(31 - NDV):
            ts("dve", DG[k], IDENT, CONVW[:, cc, k:k + 1], None, ALU.mult, None, r=["IDENT", "SM"], w=[("dg", k)])
        for k5, ((t0, w), ti) in enumerate(zip(TILES_ALL, ALL_IDS)):
            st = (cc * 5 + k5) % 2
            T1 = TC[st]
            pcv, pcg = PS[st], PS[2 + st]
            for c in range(8):
                mm(pcv[:, :w], WSC[sw][:, c, 0:128], H3[:, c, t0:t0 + w], c == 0, c == 7, r=[("wsc", sw), hr(c, ti)], w=["ps%d" % st])
            for c in range(8):
                mm(pcg[:, :w], WSC[sw][:, c, 128:256], H3[:, c, t0:t0 + w], c == 0, c == 7, r=[("wsc", sw), hr(c, ti)], w=["ps%d" % (2 + st)])
            act(T1[:, :w], pcg[:, :w], AF.Sigmoid, r=["ps%d" % (2 + st)], w=[("tc", st)])
            tt("dve", UB[sw][:, t0:t0 + w], pcv[:, :w], T1[:, :w], ALU.mult, r=["ps%d" % st, ("tc", st)], w=[("ub", sw, ti)])
        NPE = 31 - NDV
        for ti4 in range(4):
            t0 = HALO + 512 * ti4
            to = 512 * ti4
            pcn = PS[4 + ti4 % 4]
            rr = [("ub", sw, ti4 + 1), ("ub", sw, ti4)]
            acc = TC[ti4 % 2]
            for k in range(NPE, 31):
                src = UB[sw][:, t0 - 30 + k:t0 - 30 + k + 512]
                if k == NPE:
                    ts("dve", acc, src, CONVW[:, cc, k:k + 1], None, ALU.mult, None, r=rr[:2] + ["SM"], w=[("tc", ti4 % 2)])
                else:
                    stt(acc, src, CONVW[:, cc, k:k + 1], acc, ALU.mult, ALU.add, r=rr[:2] + ["SM", ("tc", ti4 % 2)], w=[("tc", ti4 % 2)])
            for k in range(NPE):
                mm(pcn, DG[k], UB[sw][:, t0 - 30 + k:t0 - 30 + k + 512], k == 0, k == NPE - 1, r=rr + [("dg", k)], w=["ps%d" % (4 + ti4 % 4)])
            stt(C3[:, cc, to:to + 512], pcn, CONVB[:, cc:cc + 1], acc, ALU.add, ALU.add,
                r=["ps%d" % (4 + ti4 % 4), "SM", ("tc", ti4 % 2)], w=[("C", cc, ti4)])
    P.barrier()

    ZW_OFF = CW_OFF
    TZ = [[f32v(ZW_OFF + (i * 3 + j) * 512, 512) for j in range(3)] for i in range(2)]
    SQZ = [bfv(ZW_OFF + 3072 + i * 256, 256) for i in range(2)]
    WHG = [bfv(ZW_OFF + 3584 + i * 512, 512).rearrange("p (c f) -> p c f", c=8) for i in range(2)]
    WSA = [bfv(ZW_OFF + 4608 + i * 512, 512).rearrange("p (c f) -> p c f", c=8) for i in range(2)]
    WSB = [bfv(ZW_OFF + 5632 + i * 512, 512).rearrange("p (c f) -> p c f", c=8) for i in range(2)]
    assert ZW_OFF + 6656 <= C_OFF3
    WSD = [bfv(C_OFF3 + 8192 + i * 512, 512).rearrange("p (c f) -> p c f", c=8) for i in range(2)]
    assert C_OFF3 + 8192 + 1024 <= X_OFF + XW

    def rstd_from(out, in_, scale, r, w):
        act(out, in_, AF.Ln, r=r, w=w, bias=EPSB, scale=scale)
        act(out, out, AF.Exp, r=w, w=w, scale=-0.5)

    T1Z = [TZ[0][0], TZ[1][0], f32v(R_OFF + 5856, 512)]
    assert R_OFF + 5856 + 512 <= NW
    T3Z = [TZ[0][2], TZ[1][2], f32v(X_OFF + XW - 512, 512)]
    assert C_OFF3 + 8192 + 1024 <= X_OFF + XW - 512 or True

    def zinfo(k):
        hd, ti4 = k // 4, k % 4
        return hd, ti4, ti4 + 1, HALO + 512 * ti4, 512 * ti4

    def Z1(k):
        hd, ti4, ti, t0, to = zinfo(k)
        s_w = hd % 2
        if ti4 == 0:
            dma("pool", WHG[s_w], whg_d[hd, :, :, 384:512], r=[], w=[("whg", s_w)], key=("whg", s_w))
        pc, pg = PS[k % 2], PS[4 + k % 3]
        mm(pc, SIN[:, hd, :], B3[:, hd, to:to + 512], True, True, r=["SIN", ("B", hd, ti4)], w=["ps%d" % (k % 2)])
        for c in range(8):
            mm(pg, WHG[s_w][:, c, :], H3[:, c, t0:t0 + 512], c == 0, c == 7, r=[("whg", s_w), hr(c, ti)], w=["ps%d" % (4 + k % 3)])
        tt("dve", T1Z[k % 3], pc, A3[:, hd, to:to + 512], ALU.add, r=["ps%d" % (k % 2), ("A", hd, ti4)], w=[("t1z", k % 3)])

    def Z2(k):
        st = k % 2
        T3 = T3Z[k % 3]
        pg = PS[4 + k % 3]
        act(SQZ[st], T1Z[k % 3], AF.Square, r=[("t1z", k % 3)], w=[("sqz", st)])
        sigmoid_act(T3, pg, r=["ps%d" % (4 + k % 3)], w=[("t3z", k % 3)])
        tt("dve", T3, pg, T3, ALU.mult, r=["ps%d" % (4 + k % 3), ("t3z", k % 3)], w=[("t3z", k % 3)])

    def Z3(k):
        hd, ti4, ti, t0, to = zinfo(k)
        st = k % 2
        T2, T3 = TZ[st][1], T3Z[k % 3]
        T1 = T1Z[k % 3]
        pn = PS[2 + st]
        mm(pn, ONES, SQZ[st], True, True, r=["ONES", ("sqz", st)], w=["ps%d" % (2 + st)])
        rstd_act(T2, pn, 1.0 / 128, r=["ps%d" % (2 + st)], w=[("tz", st, 1)])
        stt(T1, T1, HN, T2, ALU.mult, ALU.mult, r=[("t1z", k % 3), ("tz", st, 1), "SM"], w=[("t1z", k % 3)])
        tt("dve", A3[:, hd, to:to + 512], T1, T3, ALU.mult, r=[("t1z", k % 3), ("t3z", k % 3)], w=[("A", hd, ti4)])

    for i in range(34):
        if i < 32:
            Z1(i)
        if 0 <= i - 1 < 32:
            Z2(i - 1)
        if 0 <= i - 2 < 32:
            Z3(i - 2)
    P.barrier()

    MEAN = f32v(R_OFF, T)
    LRS = f32v(R_OFF + T, T)
    for ti4 in range(4):
        to = 512 * ti4
        st = ti4 % 2
        pss, psq = PS[st], PS[2 + st]
        for cc in range(8):
            mm(pss, ONES, C3[:, cc, to:to + 512], cc == 0, cc == 7, r=["ONES", ("C", cc, ti4)], w=["ps%d" % st])
        for cc in range(8):
            sq = SQZ[cc % 2]
            act(sq, C3[:, cc, to:to + 512], AF.Square, r=[("C", cc, ti4)], w=[("sqz", cc % 2)])
            mm(psq, ONES, sq, cc == 0, cc == 7, r=["ONES", ("sqz", cc % 2)], w=["ps%d" % (2 + st)])
        T1 = TZ[st][0]
        act(MEAN[:, to:to + 512], pss, AF.Copy, r=["ps%d" % st], w=[("mean", ti4)], scale=1.0 / D)
        tt("dve", T1, MEAN[:, to:to + 512], MEAN[:, to:to + 512], ALU.mult, r=[("mean", ti4)], w=[("tz", st, 0)])
        stt(T1, psq, 1.0 / D, T1, ALU.mult, ALU.subtract, r=["ps%d" % (2 + st), ("tz", st, 0)], w=[("tz", st, 0)])
        act(LRS[:, to:to + 512], T1, AF.Ln, r=[("tz", st, 0)], w=[("lrs", ti4)], bias=EPSB, scale=1.0)
        act(LRS[:, to:to + 512], LRS[:, to:to + 512], AF.Exp, r=[("lrs", ti4)], w=[("lrs", ti4)], scale=-0.5)
    def ln_apply(ti4):
        to = 512 * ti4
        for cc in range(8):
            st = cc % 2
            T1 = TZ[st][2]
            tt("dve", T1, C3[:, cc, to:to + 512], MEAN[:, to:to + 512], ALU.subtract, r=[("C", cc, ti4), ("mean", ti4)], w=[("tz", st, 2)])
            tt("pool", T1, T1, LRS[:, to:to + 512], ALU.mult, r=[("tz", st, 2), ("lrs", ti4)], w=[("tz", st, 2)])
            act(C3[:, cc, to:to + 512], T1, AF.Silu, r=[("tz", st, 2), "SM"], w=[("C", cc, ti4)],
                bias=LNB[:, cc:cc + 1], scale=LNG[:, cc:cc + 1])

    def load_merged(o):
        sw = o % 2
        dma("pool", WSA[sw], wo_d[o], r=[], w=[("wsa", sw)], key=("wsa", sw))
        dma("pool", WHG[sw], wpw_d[o], r=[], w=[("whg", sw)], key=("whg", sw))
        dma("pool", WSB[sw], wgab_d[o][:, :, 0:128], r=[], w=[("wsb", sw)], key=("wsb", sw))
        dma("pool", WSD[sw], wgab_d[o][:, :, 128:256], r=[], w=[("wsd", sw)], key=("wsd", sw))

    load_merged(0)
    load_merged(1)
    WG2 = bfv(R_OFF, 2048).rearrange("p (c f) -> p c f", c=8)
    WU2 = bfv(R_OFF + 2048, 2048).rearrange("p (c f) -> p c f", c=8)
    lnres = [("mean", t_) for t_ in range(4)] + [("lrs", t_) for t_ in range(4)]
    WOVR[(1, 0)] = (WG2, WU2, "wg2", "wu2")
    for o in range(8):
        sw = o % 2
        if o == 1:
            dma("pool", WG2, wg_d[1][:, :, 0:512], r=[], w=["wg2"] + lnres, key="wg2")
            dma("pool", WU2, wu_d[1][:, :, 0:512], r=[], w=["wu2"] + lnres, key="wu2")
        if 1 <= o and o + 1 < 8:
            load_merged(o + 1)
        for ti4 in range(4):
            if o == 0:
                ln_apply(ti4)
            st = (o * 4 + ti4) % 2
            ti = ti4 + 1
            t0 = HALO + 512 * ti4
            to = 512 * ti4
            T1, T2, T3 = TZ[st]
            pya, pga, pyb, pgb = PS[st], PS[2 + st], PS[4 + st], PS[6 + st]
            for h in range(8):
                mm(pya, WSA[sw][:, h, :], A3[:, h, to:to + 512], h == 0, h == 7, r=[("wsa", sw), ("A", h, ti4)], w=["ps%d" % st])
            for c in range(8):
                mm(pga, WSB[sw][:, c, :], H3[:, c, t0:t0 + 512], c == 0, c == 7, r=[("wsb", sw), hr(c, ti)], w=["ps%d" % (2 + st)])
            for c in range(8):
                mm(pgb, WSD[sw][:, c, :], H3[:, c, t0:t0 + 512], c == 0, c == 7, r=[("wsd", sw), hr(c, ti)], w=["ps%d" % (6 + st)])
            for cc in range(8):
                mm(pyb, WHG[sw][:, cc, :], C3[:, cc, to:to + 512], cc == 0, cc == 7, r=[("whg", sw), ("C", cc, ti4)], w=["ps%d" % (4 + st)])
            act(T1, pga, AF.Sigmoid, r=["ps%d" % (2 + st)], w=[("tz", st, 0)])
            act(T2, pgb, AF.Sigmoid, r=["ps%d" % (6 + st)], w=[("tz", st, 1)])
            tt("dve", T1, pya, T1, ALU.mult, r=["ps%d" % st, ("tz", st, 0)], w=[("tz", st, 0)])
            stt(T2, pyb, PWB[:, o:o + 1], T2, ALU.add, ALU.mult, r=["ps%d" % (4 + st), ("tz", st, 1), "SM"], w=[("tz", st, 1)])
            tt("pool", B3[:, o, to:to + 512], T1, T2, ALU.add, r=[("tz", st, 0), ("tz", st, 1)], w=[("B", o, ti4)])
    P.barrier()

    for c in range(8):
        dma("sp", X3[:, c, :], xsp_d[:, c * TT:(c + 1) * TT], r=[("xsp", c)], w=[xr(c, ti) for ti in range(5)], key=("xrl", c))
    WS2 = [bfv(R_OFF + 4096 + i * 512, 512).rearrange("p (c f) -> p c f", c=8) for i in range(2)]
    for o in range(8):
        sw = o % 2
        dma("pool", WS2[sw], wout_d[o], r=[], w=[("ws2", sw)], key=("ws2", sw))
        for ti4 in range(4):
            st = (o * 4 + ti4) % 4
            ti = ti4 + 1
            t0 = HALO + 512 * ti4
            to = 512 * ti4
            px = PS[st]
            for m in range(8):
                mm(px, WS2[sw][:, m, :], B3[:, m, to:to + 512], m == 0, m == 7, r=[("ws2", sw), ("B", m, ti4)], w=["ps%d" % st])
            tt("dve", X3[:, o, t0:t0 + 512], px, X3[:, o, t0:t0 + 512], ALU.add, r=["ps%d" % st, xr(o, ti)], w=[xr(o, ti)])
    P.barrier()

    if DEBUG_STAGE == "mix":
        dump_X()
        P.op("sp", None, r=[("y", c) for c in range(8)], w=[])
        return nc, P, (arena_cm, ps_cms)

    ffn(1, TILES_OWN, OWN_IDS, G3)
    P.barrier()
    if DEBUG_STAGE == "ffn2":
        dump_X()
        P.op("sp", None, r=[("y", c) for c in range(8)], w=[])
        return nc, P, (arena_cm, ps_cms)
    for k, ((t0, w), ti) in enumerate(zip(TILES_OWN, OWN_IDS)):
        pn = PS[7]
        for c in range(8):
            sq = FM["sq"][c % 4]
            act(sq, X3[:, c, t0:t0 + w], AF.Square, r=[xr(c, ti)], w=[("sq", c % 4)])
            mm(pn, ONES, sq, c == 0, c == 7, r=["ONES", ("sq", c % 4)], w=["ps7"])
        rstd_act(RSTD[:, t0:t0 + w], pn, 1.0 / D, r=["ps7"], w=[("rstd", ti)])
        ost = FM["ost"][k % 2]
        for c in range(8):
            stt(ost[:, c, :], X3[:, c, t0:t0 + w], GF[:, c:c + 1], RSTD[:, t0:t0 + w], ALU.mult, ALU.mult,
                r=[xr(c, ti), ("rstd", ti), "SM"], w=[("ost", k % 2)])
        dma("sp", y_d[:, :, (t0 - HALO):(t0 - HALO) + 512], ost, r=[("ost", k % 2)], w=[("y", k)], key="out")
    P.op("sp", None, r=[("y", k) for k in range(4)], w=[])
    return nc, P, (arena_cm, ps_cms)


def emit_program(nc, P):
    keys = []
    for o in P.ops:
        if o.key is not None and o.key not in keys:
            keys.append(o.key)
    sem_cms = []

    def newsem(name):
        cm = nc.semaphore(name)
        sem_cms.append(cm)
        return cm.__enter__()

    esems = {e: newsem("se_" + e) for e in Prog.ENGS}
    dsems = {k: newsem("sd_%d" % i) for i, k in enumerate(keys)}
    streams = P.emit(nc, None, esems, dsems)
    with nc.Block() as block:
        @block.tensor
        def _(e):
            run_stream(streams["pe"], e, esems, dsems)

        @block.scalar
        def _(e):
            run_stream(streams["act"], e, esems, dsems)

        @block.vector
        def _(e):
            run_stream(streams["dve"], e, esems, dsems)

        @block.gpsimd
        def _(e):
            run_stream(streams["pool"], e, esems, dsems)

        @block.sync
        def _(e):
            run_stream(streams["sp"], e, esems, dsems)
    return nc


def _pm(w, nchunk):
    return np.ascontiguousarray(w.reshape(nchunk, 128, -1).transpose(1, 0, 2))


def _colblocks(w_pm, starts, width):
    return np.ascontiguousarray(np.concatenate([w_pm[:, :, s:s + width] for s in starts], axis=2))


def kernel(x, ffn1_norm, ffn1_w_gate, ffn1_w_up, ffn1_w_down, mix_norm, w_in, hgrn_lb_logits, hgrn_head_norm,
           hgrn_w_o, conv_w, conv_b, conv_ln_g, conv_ln_b, conv_w_pw, conv_b_pw, w_out, ffn2_norm, ffn2_w_gate,
           ffn2_w_up, ffn2_w_down, final_norm):
    f = lambda a: np.asarray(a, dtype=np.float32)
    x = f(x)
    win = _pm(f(w_in)[0], 8)
    shared = {
        "wg1": _pm(f(ffn1_w_gate)[0], 8), "wu1": _pm(f(ffn1_w_up)[0], 8), "wd1": _pm(f(ffn1_w_down)[0], NFC),
        "wg2": _pm(f(ffn2_w_gate)[0], 8), "wu2": _pm(f(ffn2_w_up)[0], 8), "wd2": _pm(f(ffn2_w_down)[0], NFC),
        "whg": np.stack([_colblocks(win, [h * 128, 1024 + h * 128, 2048 + h * 128, 3072 + h * 128], 128) for h in range(8)]),
        "wcv": np.stack([_colblocks(win, [4096 + c * 128, 5120 + c * 128], 128) for c in range(8)]),
        "wgab": np.stack([_colblocks(win, [6144 + c * 128, 7168 + c * 128], 128) for c in range(8)]),
    }

    def oblocks(w):
        wp = _pm(f(w)[0], 8)
        return np.stack([np.ascontiguousarray(wp[:, :, o * 128:(o + 1) * 128]) for o in range(8)])

    shared["wo"] = oblocks(hgrn_w_o)
    shared["wpw"] = oblocks(conv_w_pw)
    shared["wout"] = oblocks(w_out)

    def col8(v):
        return f(v).reshape(8, 128).T

    NSM = 8 * 13 + 2 + 8 * 31 + 2 + 24
    small = np.zeros((128, NSM), np.float32)
    small[:, 0:8] = col8(ffn1_norm[0])
    small[:, 8:16] = col8(mix_norm[0])
    small[:, 16:24] = col8(ffn2_norm[0])
    small[:, 24:32] = col8(final_norm)
    lbl = f(hgrn_lb_logits)
    small[:, 32:40] = col8(lbl[0])
    small[:, 40:48] = col8(lbl[1])
    small[:, 48:56] = col8(conv_b[0])
    small[:, 56:64] = col8(conv_ln_g[0])
    small[:, 64:72] = col8(conv_ln_b[0])
    small[:, 72:80] = col8(conv_b_pw[0])
    small[:, 96] = f(hgrn_head_norm)[0]
    small[:, 97] = EPS
    small[:, 354] = 1.0
    cw = f(conv_w)[0]
    small[:, 106:106 + 248] = cw.T.reshape(8, 128, 31).transpose(1, 0, 2).reshape(128, 248)
    shared["small"] = small

    cst = np.zeros((128, 128 * 2 + 512 * 3), np.float32)
    cst[:, 0:128] = np.eye(128, dtype=np.float32)
    cst[:, 128:256] = 1.0
    s_idx = np.arange(128)[:, None]
    t_idx = np.arange(128)[None, :]
    cm = ((s_idx // 64 == t_idx // 64) & (s_idx <= t_idx)).astype(np.float32)
    cst[:, 256:768] = np.tile(cm, (1, 4))
    rs = np.ones(512, np.float32)
    rs[0::64] = 0.0
    cst[:, 768:1280] = rs[None, :]
    cst[:, 1280:1792] = 1.0
    shared["cst"] = cst

    in_maps = []
    for c in range(NCORES):
        b, j = c // 4, c % 4
        own = x[b, j * T:(j + 1) * T, :]
        halo = x[b, j * T - HALO:j * T, :] if j > 0 else np.zeros((HALO, D), np.float32)
        xc = np.concatenate([halo, own], axis=0).T
        xc = xc.reshape(8, 128, TT).transpose(1, 0, 2).reshape(128, 8 * TT)
        inc = np.zeros((128, 16), np.float32)
        for r_ in range(4):
            v = 1.0 if r_ < j else 0.0
            inc[:, r_] = v
            inc[:, 8 + r_] = 1.0 - v
        m = dict(shared)
        m["xT"] = np.ascontiguousarray(xc)
        m["inc"] = inc
        in_maps.append(m)

    nc, P, keep = build_program()
    emit_program(nc, P)
    res = run_bass_kernel_spmd(nc, in_maps, core_ids=list(range(NCORES)))
    out = np.empty((2, 4 * T, D), np.float32)
    for c in range(NCORES):
        b, j = c // 4, c % 4
        yc = np.asarray(res.results[c]["y"]).reshape(128, 8, T)
        out[b, j * T:(j + 1) * T, :] = yc.transpose(2, 1, 0).reshape(T, D)
    return out
```

```python
import bisect
import numpy as np
import concourse.bass as bass
import concourse.mybir as mybir
from concourse.bass_utils import run_bass_kernel_spmd

F32 = mybir.dt.float32
BF16 = mybir.dt.bfloat16
AF = mybir.ActivationFunctionType
ALU = mybir.AluOpType

NCORES = 8
D = 1024
DFF = 2816
NFC = 22
T = 2048
HALO = 32
TT = T + HALO
EPS = 1e-6
TILES_ALL = [(0, HALO)] + [(HALO + 512 * i, 512) for i in range(4)]
TILES_OWN = TILES_ALL[1:]
FGROUPS = [(0, 4), (4, 4), (8, 4), (12, 4), (16, 4), (20, 2)]

DEBUG_STAGE = "full"
NO_CC = False


class Op:
    __slots__ = ("eng", "fn", "deps", "signal", "ord", "key", "gidx", "unit")

    def __init__(self, eng, fn, key=None, unit=16):
        self.eng = eng
        self.fn = fn
        self.deps = set()
        self.signal = False
        self.ord = 0
        self.key = key
        self.gidx = 0
        self.unit = unit


class Prog:
    ENGS = ("pe", "act", "dve", "pool", "sp")

    def __init__(self):
        self.ops = []
        self.wlast = {}
        self.rlast = {}
        self.pending_bar = {}
        self.last_on_eng = {}
        self.last_dma = {}

    def op(self, eng, fn, r=(), w=(), key=None, unit=16):
        o = Op(eng, fn, key, unit)
        o.gidx = len(self.ops)
        is_dma = key is not None
        deps = set()
        for res in r:
            for p in self.wlast.get(res, {}).values():
                deps.add((p, "raw"))
        for res in w:
            for p in self.wlast.get(res, {}).values():
                deps.add((p, "waw"))
            for p in self.rlast.get(res, {}).values():
                deps.add((p, "war"))
        for (p, kind) in deps:
            if p is o:
                continue
            if p.eng == eng and p.key is None and not is_dma:
                if eng == "pe":
                    continue
            o.deps.add(p)
        if eng in self.pending_bar:
            o.deps |= self.pending_bar.pop(eng)
        best = {}
        keep = set()
        for p in o.deps:
            if p.key is None:
                if p.eng not in best or p.gidx > best[p.eng].gidx:
                    best[p.eng] = p
            else:
                keep.add(p)
        o.deps = keep | set(best.values())
        th = o.eng if key is None else ("dma", key)
        for res in r:
            self.rlast.setdefault(res, {})[th] = o
        for res in w:
            self.wlast.setdefault(res, {})[th] = o
        self.ops.append(o)
        if is_dma:
            self.last_dma[key] = o
        else:
            self.last_on_eng[eng] = o
        return o

    def barrier(self):
        allp = set(self.last_on_eng.values()) | set(self.last_dma.values())
        for e in self.ENGS:
            self.pending_bar[e] = set(allp) | self.pending_bar.get(e, set())

    def emit(self, nc, ehandles, esems, dsems):
        for o in self.ops:
            for p in o.deps:
                if p.key is None:
                    p.signal = True
        cnt = {e: 0 for e in self.ENGS}
        keyidx = {}
        for o in self.ops:
            if o.key is None:
                if o.signal:
                    cnt[o.eng] += 1
                o.ord = cnt[o.eng]
            else:
                keyidx.setdefault(o.key, []).append(o.gidx)
        waited = {}
        streams = {e: [] for e in self.ENGS}
        for o in self.ops:
            waits = {}
            for p in o.deps:
                if p.key is None:
                    sem = ("e", p.eng)
                    val = p.ord
                else:
                    sem = ("d", p.key)
                    val = p.unit * bisect.bisect_left(keyidx[p.key], o.gidx)
                if val > waits.get(sem, 0):
                    waits[sem] = val
            wl = []
            for sem, val in waits.items():
                if waited.get((o.eng, sem), 0) >= val:
                    continue
                waited[(o.eng, sem)] = val
                wl.append((sem, val))
            streams[o.eng].append((o, wl))
        return streams


def run_stream(stream, eh, esems, dsems):
    for (o, wl) in stream:
        for (sem, val) in wl:
            s = esems[sem[1]] if sem[0] == "e" else dsems[sem[1]]
            eh.wait_ge(s, val)
        if o.fn is None:
            continue
        ins = o.fn(eh)
        if o.key is not None:
            if o.unit == 16:
                ins.then_inc(dsems[o.key], 16)
            else:
                ins.then_inc(dsems[o.key], 1)
        elif o.signal:
            ins.then_inc(esems[o.eng], 1)


def build_program():
    nc = bass.Bass("TRN2", target_bir_lowering=False)

    def din(name, shape):
        return nc.dram_tensor(name, list(shape), F32, kind="ExternalInput").ap()

    xT = din("xT", [128, 8 * TT])
    wg_d = [din("wg1", [128, 8, DFF]), din("wg2", [128, 8, DFF])]
    wu_d = [din("wu1", [128, 8, DFF]), din("wu2", [128, 8, DFF])]
    wd_d = [din("wd1", [128, NFC, D]), din("wd2", [128, NFC, D])]
    whg_d = din("whg", [8, 128, 8, 512])
    wcv_d = din("wcv", [8, 128, 8, 256])
    wgab_d = din("wgab", [8, 128, 8, 256])
    wo_d = din("wo", [8, 128, 8, 128])
    wpw_d = din("wpw", [8, 128, 8, 128])
    wout_d = din("wout", [8, 128, 8, 128])
    NSM = 8 * 13 + 2 + 8 * 31 + 2 + 24
    small_d = din("small", [128, NSM])
    cst_d = din("cst", [128, 128 * 2 + 512 * 3])
    inc_d = din("inc", [128, 16])
    y_d = nc.dram_tensor("y", [128, 8, T], F32, kind="ExternalOutput").ap()
    xsp_d = nc.dram_tensor("xsp", [128, 8 * TT], F32, kind="Internal").ap()
    GW = 8 * 128 + 8
    gsrc_d = nc.dram_tensor("gsrc", [128, GW], F32, kind="Internal").ap()
    gdst_d = nc.dram_tensor("gdst", [4 * 128, GW], F32, kind="Internal").ap()

    NW = 52000
    P = Prog()

    arena_cm = nc.sbuf_tensor("arena", [128, NW], F32)
    arena_t = arena_cm.__enter__()
    AR = arena_t[:]
    ps_cms = [nc.psum_tensor("ps%d" % i, [128, 512], F32) for i in range(8)]
    PS = [cm.__enter__()[:] for cm in ps_cms]

    def f32v(off, n):
        return AR[:, off:off + n]

    def bfv(off, nwords):
        return AR[:, off:off + nwords].bitcast(BF16)

    X_OFF = 0
    XW = 8 * TT
    H_OFF = X_OFF + XW
    HW = 8 * TT // 2
    C_OFF = H_OFF + HW
    P_OFF = C_OFF + 4224
    X3 = f32v(X_OFF, XW).rearrange("p (c t) -> p c t", c=8)
    H3 = bfv(H_OFF, HW).rearrange("p (c t) -> p c t", c=8)

    co = [C_OFF]

    def calloc(n):
        o = co[0]
        co[0] += n
        assert co[0] <= P_OFF
        return o

    SM = f32v(calloc(NSM), NSM)
    G1, G2, G3, GF = SM[:, 0:8], SM[:, 8:16], SM[:, 16:24], SM[:, 24:32]
    LB0, LB1 = SM[:, 32:40], SM[:, 40:48]
    CONVB, LNG, LNB, PWB = SM[:, 48:56], SM[:, 56:64], SM[:, 64:72], SM[:, 72:80]
    LB, OML = SM[:, 80:88], SM[:, 88:96]
    HN = SM[:, 96:97]
    LBD = SM[:, 98:106]
    CONVW = SM[:, 106:106 + 248].rearrange("p (c k) -> p c k", c=8)
    INC = f32v(calloc(16), 16)
    IDENT = bfv(calloc(64), 64)
    ONES = bfv(calloc(64), 64)
    CMASK4 = f32v(calloc(512), 512)
    RESET = f32v(calloc(512), 512)
    ONES32 = f32v(calloc(512), 512)
    RSTD = f32v(calloc(TT), TT)
    BTOT = f32v(calloc(8), 8)

    def ffn_map():
        o = P_OFF
        m = {}
        m["wg"] = [bfv(o, 2048).rearrange("p (c f) -> p c f", c=8), bfv(o + 2048, 2048).rearrange("p (c f) -> p c f", c=8)]
        o += 4096
        m["wu"] = [bfv(o, 2048).rearrange("p (c f) -> p c f", c=8), bfv(o + 2048, 2048).rearrange("p (c f) -> p c f", c=8)]
        o += 4096
        m["wd"] = [bfv(o, 2048).rearrange("p (j d) -> p j d", j=4), bfv(o + 2048, 2048).rearrange("p (j d) -> p j d", j=4)]
        o += 4096
        m["a"] = [[bfv(o + (ab * 4 + j) * 256, 256) for j in range(4)] for ab in range(2)]
        o += 2048
        m["sl"] = [f32v(o, 512), f32v(o + 512, 512)]
        o += 1024
        m["sq"] = [bfv(o + i * 256, 256) for i in range(4)]
        o += 1024
        assert o <= NW
        m["ost"] = [f32v(P_OFF, 4096).rearrange("p (c t) -> p c t", c=8), f32v(P_OFF + 4096, 4096).rearrange("p (c t) -> p c t", c=8)]
        return m

    FM = ffn_map()

    def dma(eng, out, in_, r, w, key):
        P.op(eng, lambda e, out=out, in_=in_: e.dma_start(out=out, in_=in_), r=r, w=w, key=key)

    def mm(out, lhsT, rhs, start, stop, r, w):
        P.op("pe", lambda e, out=out, lhsT=lhsT, rhs=rhs, start=start, stop=stop:
             e.matmul(out, lhsT, rhs, start=start, stop=stop), r=r, w=w)

    def act(out, in_, func, r, w, bias=None, scale=None):
        def fn(e, out=out, in_=in_, func=func, bias=bias, scale=scale):
            kw = {}
            if bias is not None:
                kw["bias"] = bias
            if scale is not None:
                kw["scale"] = scale
            return e.activation(out, in_, func, **kw)
        P.op("act", fn, r=r, w=w)

    def tt(eng, out, in0, in1, op, r, w):
        P.op(eng, lambda e, out=out, in0=in0, in1=in1, op=op: e.tensor_tensor(out, in0, in1, op), r=r, w=w)

    def ts(eng, out, in0, s1, s2, op0, op1, r, w):
        if op1 is None:
            P.op(eng, lambda e, out=out, in0=in0, s1=s1, op0=op0: e.tensor_scalar(out, in0, s1, None, op0), r=r, w=w)
        else:
            P.op(eng, lambda e, out=out, in0=in0, s1=s1, s2=s2, op0=op0, op1=op1:
                 e.tensor_scalar(out, in0, s1, s2, op0, op1), r=r, w=w)

    def stt(out, in0, scalar, in1, op0, op1, r, w):
        P.op("dve", lambda e, out=out, in0=in0, scalar=scalar, in1=in1, op0=op0, op1=op1:
             e.scalar_tensor_tensor(out, in0, scalar, in1, op0, op1), r=r, w=w)

    def cp(eng, out, in_, r, w):
        if eng == "act":
            act(out, in_, AF.Copy, r, w)
        else:
            P.op(eng, lambda e, out=out, in_=in_: e.tensor_copy(out, in_), r=r, w=w)

    def xr(c, ti):
        return ("X", c, ti)

    def hr(c, ti):
        return ("H", c, ti)

    dma("sp", SM, small_d, r=[], w=["SM"], key="cst")
    dma("sp", INC, inc_d, r=[], w=["INC"], key="cst")
    dma("pool", ONES, cst_d[:, 128:256], r=[], w=["ONES"], key="cstb")
    xT3 = xT.rearrange("p (c t) -> p c t", c=8)
    def load_x_tile(ti, extra_r=()):
        t0, w = TILES_ALL[ti]
        dma("sp", X3[:, :, t0:t0 + w], xT3[:, :, t0:t0 + w], r=list(extra_r), w=[xr(c, ti) for c in range(8)], key=("xin", ti))

    load_x_tile(0)
    load_x_tile(1)
    DEFER_X = [True]
    dma("pool", IDENT, cst_d[:, 0:128], r=[], w=["IDENT"], key="cstb2")
    dma("sp", CMASK4, cst_d[:, 256:768], r=[], w=["CMASK"], key="cst2")
    dma("sp", RESET, cst_d[:, 768:1280], r=[], w=["RESET"], key="cst2")
    dma("sp", ONES32, cst_d[:, 1280:1792], r=[], w=["ONES32"], key="cst2")
    tt("dve", LBD, LB0, LB1, ALU.subtract, r=["SM"], w=["LBD"])
    act(LB, LBD, AF.Sigmoid, r=["LBD"], w=["LB"])
    ts("dve", OML, LB, -1.0, 1.0, ALU.mult, ALU.add, r=["LB"], w=["OML"])
    SMIN = SM[:, 356:364]
    NOML = SM[:, 364:372]
    ROML = SM[:, 372:380]
    ts("dve", NOML, OML, -1.0, None, ALU.mult, None, r=["OML"], w=["NOML"])
    P.op("dve", lambda e: e.reciprocal(ROML, OML), r=["OML"], w=["ROML"])
    ts("dve", SMIN, LB, -1.0, 1e-6, ALU.mult, ALU.add, r=["LB"], w=["SMIN0"])
    tt("dve", SMIN, SMIN, ROML, ALU.mult, r=["SMIN0", "ROML"], w=["SMIN"])

    def rms_norm(gcols, tiles, tile_ids):
        for (t0, w), ti in zip(tiles, tile_ids):
            norm_tile(gcols, t0, w, ti)

    def norm_tile(gcols, t0, w, ti):
        if True:
            pn = PS[7]
            for c in range(8):
                sq = FM["sq"][c % 4]
                act(sq[:, :w], X3[:, c, t0:t0 + w], AF.Square, r=[xr(c, ti)], w=[("sq", c % 4)])
                mm(pn[:, :w], ONES, sq[:, :w], c == 0, c == 7, r=["ONES", ("sq", c % 4)], w=["ps7"])
            rstd_act(RSTD[:, t0:t0 + w], pn[:, :w], 1.0 / D, r=["ps7"], w=[("rstd", ti)])
            for c in range(8):
                stt(H3[:, c, t0:t0 + w], X3[:, c, t0:t0 + w], gcols[:, c:c + 1], RSTD[:, t0:t0 + w],
                    ALU.mult, ALU.mult, r=[xr(c, ti), ("rstd", ti), "SM"], w=[hr(c, ti)])

    EPSB = SM[:, 97:98]
    ONEB = SM[:, 354:355]

    def rstd_act(out, in_, scale, r, w):
        act(out, in_, AF.Ln, r=r, w=w, bias=EPSB, scale=scale)
        act(out, out, AF.Exp, r=w, w=w, scale=-0.5)

    def sigmoid_act(out, in_, r, w):
        act(out, in_, AF.Exp, r=r, w=w, scale=-1.0)
        act(out, out, AF.Ln, r=w, w=w, bias=ONEB, scale=1.0)
        act(out, out, AF.Exp, r=w, w=w, scale=-1.0)

    PRELOADED = set()
    WOVR = {}

    def ffn_load_gu(which, gi, extra_r=()):
        j0, nf = FGROUPS[gi]
        s = gi % 2
        dma("pool", FM["wg"][s][:, :, :nf * 128], wg_d[which][:, :, j0 * 128:(j0 + nf) * 128], r=list(extra_r), w=[("wg", s)], key=("wg", s))
        dma("pool", FM["wu"][s][:, :, :nf * 128], wu_d[which][:, :, j0 * 128:(j0 + nf) * 128], r=list(extra_r), w=[("wu", s)], key=("wu", s))

    def ffn(which, tiles, tile_ids, gcols):
        steps = [(gi, k) for gi in range(len(FGROUPS)) for k in range(len(tiles))]

        def load(gi):
            j0, nf = FGROUPS[gi]
            s = gi % 2
            if (which, gi) not in PRELOADED and (which, gi) not in WOVR:
                ffn_load_gu(which, gi)
            dma("pool", FM["wd"][s][:, :nf, :], wd_d[which][:, j0:j0 + nf, :], r=[], w=[("wd", s)], key=("wd", s))

        def gu(si):
            gi, k = steps[si]
            j0, nf = FGROUPS[gi]
            s = gi % 2
            ab = si % 2
            (t0, w), ti = tiles[k], tile_ids[k]
            if (which, gi) in WOVR:
                WG, WU, rg, ru = WOVR[(which, gi)]
            else:
                WG, WU, rg, ru = FM["wg"][s], FM["wu"][s], ("wg", s), ("wu", s)
            for j in range(nf):
                pg, pu = PS[j % 2], PS[2 + j % 2]
                for c in range(8):
                    mm(pg[:, :w], WG[:, c, j * 128:(j + 1) * 128], H3[:, c, t0:t0 + w], c == 0, c == 7,
                       r=[rg, hr(c, ti)], w=["ps%d" % (j % 2)])
                for c in range(8):
                    mm(pu[:, :w], WU[:, c, j * 128:(j + 1) * 128], H3[:, c, t0:t0 + w], c == 0, c == 7,
                       r=[ru, hr(c, ti)], w=["ps%d" % (2 + j % 2)])
                sl = FM["sl"][j % 2]
                act(sl[:, :w], pg[:, :w], AF.Silu, r=["ps%d" % (j % 2)], w=[("sl", j % 2)])
                tt("dve", FM["a"][ab][j][:, :w], pu[:, :w], sl[:, :w], ALU.mult,
                   r=["ps%d" % (2 + j % 2), ("sl", j % 2)], w=[("a", ab, j)])

        def dn(si):
            gi, k = steps[si]
            j0, nf = FGROUPS[gi]
            s = gi % 2
            ab = si % 2
            (t0, w), ti = tiles[k], tile_ids[k]
            for o in range(8):
                pd = PS[4 + o % 4]
                for j in range(nf):
                    mm(pd[:, :w], FM["wd"][s][:, j, o * 128:(o + 1) * 128], FM["a"][ab][j][:, :w], j == 0, j == nf - 1,
                       r=[("wd", s), ("a", ab, j)], w=["ps%d" % (4 + o % 4)])
                stt(X3[:, o, t0:t0 + w], pd[:, :w], 0.5, X3[:, o, t0:t0 + w], ALU.mult, ALU.add,
                    r=["ps%d" % (4 + o % 4), xr(o, ti)], w=[xr(o, ti)])

        load(0)
        if which == 0 and DEFER_X[0]:
            DEFER_X[0] = False
            for ti_ in range(2, 5):
                load_x_tile(ti_, extra_r=[("wu", 0)])
        for si in range(len(steps)):
            if steps[si][0] == 0:
                k0 = steps[si][1]
                norm_tile(gcols, tiles[k0][0], tiles[k0][1], tile_ids[k0])
            gu(si)
            if si == 0 and len(FGROUPS) > 1:
                load(1)
            if si > 0:
                dn(si - 1)
                gi_prev, k_prev = steps[si - 1]
                if k_prev == len(tiles) - 1 and gi_prev + 2 < len(FGROUPS):
                    load(gi_prev + 2)
        dn(len(steps) - 1)

    ALL_IDS = list(range(5))
    OWN_IDS = list(range(1, 5))
    ffn(0, TILES_ALL, ALL_IDS, G1)

    out_cnt = [0]

    def dump_X():
        P.barrier()
        for c in range(8):
            dma("sp", y_d[:, c, :], X3[:, c, HALO:HALO + T], r=[xr(c, ti) for ti in range(5)], w=[("y", c)], key="out")
            out_cnt[0] += 1

    def finish():
        P.op("sp", None, r=[("y", c) for c in range(8)] + ["yfin"], w=[])
        P.barrier()

    if DEBUG_STAGE == "ffn1":
        dump_X()
        P.op("sp", None, r=[("y", c) for c in range(8)], w=[])
        return nc, P, (arena_cm, ps_cms)

    rms_norm(G2, TILES_ALL, ALL_IDS)
    for c in range(8):
        dma("sp", xsp_d[:, c * TT:(c + 1) * TT], X3[:, c, :], r=[xr(c, ti) for ti in range(5)], w=[("xsp", c)], key=("xsp", c))
    P.barrier()

    A_OFF = P_OFF
    B_OFF = P_OFF + 8192
    R_OFF = P_OFF + 16384
    A3 = bfv(A_OFF, 8192).rearrange("p (c t) -> p c t", c=8)
    B3 = bfv(B_OFF, 8192).rearrange("p (c t) -> p c t", c=8)

    xo = [X_OFF]

    def xalloc(n):
        o = xo[0]
        xo[0] += n
        assert xo[0] <= X_OFF + XW, xo[0]
        return o

    ro = [R_OFF]

    def ralloc(n):
        o = ro[0]
        ro[0] += n
        assert ro[0] <= NW, ro[0]
        return o

    WH = [bfv(xalloc(2048), 2048).rearrange("p (c f) -> p c f", c=8) for _ in range(2)]
    TF = [[f32v(xalloc(512), 512) for _ in range(5)] for _ in range(2)]
    TB = [[bfv(xalloc(256), 256) for _ in range(6)] for _ in range(3)]
    SALL = [f32v(xalloc(1024), 1024).rearrange("p (n d) -> p n d", n=8) for _ in range(2)]
    GS = f32v(ralloc(GW), GW)
    GS3 = GS[:, 0:1024].rearrange("p (h d) -> p h d", h=8)
    BT = f32v(ralloc(64), 64)
    BT3 = BT.rearrange("p (r h) -> p r h", r=8)
    NR = 4
    EB_ = f32v(ralloc(64), 64)
    CR = f32v(ralloc(64), 64)
    RS = f32v(ralloc(1024), 1024)
    RS3 = RS.rearrange("p (h d) -> p h d", h=8)
    GL = [f32v(ralloc(1024), 1024) for _ in range(2)]
    SIN = bfv(ralloc(512), 512).rearrange("p (h d) -> p h d", h=8)
    SBF = [bfv(ralloc(512), 512).rearrange("p (n d) -> p n d", n=8) for _ in range(2)]
    EBL = [f32v(ralloc(8), 8) for _ in range(3)]

    pq, pf, pv, ptr, pA, pU0, pU1, po = PS
    ptr_bf = ptr.bitcast(BF16)
    NT = 32

    def tinfo(k):
        hd, ti4 = k // 4, k % 4
        return hd, ti4, ti4 + 1, HALO + 512 * ti4, 512 * ti4

    def load_wh(hd):
        s_w = hd % 2
        dma("pool", WH[s_w][:, :, 0:384], whg_d[hd, :, :, 0:384], r=[], w=[("wh", s_w)], key=("wh", s_w))

    def F1(k):
        hd, ti4, ti, t0, to = tinfo(k)
        s_w, f, b = hd % 2, k % 2, k % 3
        if ti4 == 0 and hd + 1 < 8:
            load_wh(hd + 1)
        for c in range(8):
            mm(pq, WH[s_w][:, c, 0:128], H3[:, c, t0:t0 + 512], c == 0, c == 7, r=[("wh", s_w), hr(c, ti)], w=["ps0"])
        for s in range(4):
            for c in range(8):
                mm(pv[:, s * 128:(s + 1) * 128], H3[:, c, t0 + s * 128:t0 + (s + 1) * 128], WH[s_w][:, c, 256:384],
                   c == 0, c == 7, r=[("wh", s_w), hr(c, ti)], w=["ps2"])

    def F1f(k):
        hd, ti4, ti, t0, to = tinfo(k)
        s_w = hd % 2
        for c in range(8):
            mm(pf, WH[s_w][:, c, 128:256], H3[:, c, t0:t0 + 512], c == 0, c == 7, r=[("wh", s_w), hr(c, ti)], w=["ps1"])

    def S1(k):
        f, b = k % 2, k % 3
        T1 = TF[f][0]
        VTOK = TB[b][4]
        sigmoid_act(T1, pf, r=["ps1"], w=[("tf", f, 0)])

    def E0(k):
        hd, ti4, ti, t0, to = tinfo(k)
        f = k % 2
        T1, T2 = TF[f][0], TF[f][1]
        rT = [("tf", f, i) for i in range(5)]
        ts("dve", T1, T1, SMIN[:, hd:hd + 1], None, ALU.max, None, r=[rT[0], "SMIN"], w=[rT[0]])
        act(T2, T1, AF.Ln, r=[rT[0], "LB", "OML"], w=[rT[1]], bias=LB[:, hd:hd + 1], scale=OML[:, hd:hd + 1])

    def E(k):
        hd, ti4, ti, t0, to = tinfo(k)
        f, b = k % 2, k % 3
        fo = 1 - f
        T1, T2, T3, T4, T5 = TF[f]
        QE, KE, KHT, KHTOK, VTOK, AM = TB[b]
        rT = [("tf", f, i) for i in range(5)]
        rB = [("tb", b, i) for i in range(6)]
        P.op("dve", lambda e, o=T3, d1=T2: e.tensor_tensor_scan(o, RESET, d1, 0.0, ALU.mult, ALU.add),
             r=[rT[1], "RESET"], w=[rT[2]])
        if ti4 == 0:
            P.op("dve", lambda e, o=T4, d1=T2: e.tensor_tensor_scan(o, ONES32, d1, 0.0, ALU.mult, ALU.add),
                 r=[rT[1], "ONES32"], w=[rT[3]])
        else:
            prevT4 = TF[fo][3]
            P.op("dve", lambda e, o=T4, d1=T2, ini=prevT4[:, 511:512]: e.tensor_tensor_scan(o, ONES32, d1, ini, ALU.mult, ALU.add),
                 r=[rT[1], "ONES32", ("tf", fo, 3)], w=[rT[3]])
        ts("pool", T1, T1, NOML[:, hd:hd + 1], OML[:, hd:hd + 1], ALU.mult, ALU.add, r=[rT[0], "NOML", "OML"], w=[rT[0]])

    def Ec1(k):
        f = k % 2
        T1, T2, T3, T4, T5 = TF[f]
        rT = [("tf", f, i) for i in range(5)]
        act(T5, T3, AF.Exp, r=[rT[2]], w=[rT[4]])

    def Ec(k):
        f = k % 2
        T1, T2, T3, T4, T5 = TF[f]
        rT = [("tf", f, i) for i in range(5)]
        act(T2, T4, AF.Exp, r=[rT[3], rT[1]], w=[rT[1]])
        act(T3, T3, AF.Exp, r=[rT[2]], w=[rT[2]], scale=-1.0)

    def Eb(k):
        hd, ti4, ti, t0, to = tinfo(k)
        f, b = k % 2, k % 3
        T1, T2, T3, T4, T5 = TF[f]
        QE, KE, KHT, KHTOK, VTOK, AM = TB[b]
        rT = [("tf", f, i) for i in range(5)]
        rB = [("tb", b, i) for i in range(6)]
        cp("act", VTOK, pv, r=["ps2"], w=[("tb", b, 4)])
        tt("dve", QE, pq, T5, ALU.mult, r=["ps0", rT[4]], w=[rB[0]])
        tt("dve", B3[:, hd, to:to + 512], pq, T2, ALU.mult, r=["ps0", rT[1]], w=[("B", hd, ti4)])
        tt("pool", KE, T1, T3, ALU.mult, r=[rT[0], rT[2]], w=[rB[1]])
        tt("pool", KHT.rearrange("p (n t) -> p n t", n=8), KE.rearrange("p (n t) -> p n t", n=8),
           T5[:, 63:512:64].to_broadcast([128, 8, 64]), ALU.mult, r=[rB[1], rT[4]], w=[rB[2]])
        cp("pool", EBL[b], T5[:, 63:512:64], r=[rT[4]], w=[("ebl", b)])
        if ti4 == 3:
            cp("pool", GS[:, 1024 + hd:1025 + hd], T4[:, 511:512], r=[rT[3]], w=["GS"])

    def F2(k):
        b = k % 3
        QE, KE, KHT, KHTOK, VTOK, AM = TB[b]
        rB = [("tb", b, i) for i in range(6)]
        for s in range(4):
            P.op("pe", lambda e, o=ptr_bf[:, s * 128:(s + 1) * 128], i=KHT[:, s * 128:(s + 1) * 128]: e.transpose(o, i, IDENT),
                 r=[rB[2], "IDENT"], w=["ps3"])
        cp("act", KHTOK, ptr_bf[:, 0:512], r=["ps3"], w=[rB[3]])
        for s in range(4):
            mm(pA[:, s * 128:(s + 1) * 128], KE[:, s * 128:(s + 1) * 128], QE[:, s * 128:(s + 1) * 128], True, True,
               r=[rB[0], rB[1]], w=["ps4"])
        tt("dve", AM, pA, CMASK4, ALU.mult, r=["ps4", "CMASK"], w=[rB[5]])
        for n in range(8):
            s, half = n // 2, n % 2
            r0 = half * 64
            pu = pU0 if half == 0 else pU1
            mm(pu[:, s * 128:(s + 1) * 128], KHTOK[r0:r0 + 64, s * 128:(s + 1) * 128],
               VTOK[r0:r0 + 64, s * 128:(s + 1) * 128], True, True, r=[rB[3], rB[4]], w=["ps%d" % (5 + half)])

    def Bk(k):
        hd, ti4, ti, t0, to = tinfo(k)
        b, st = k % 3, k % 2
        so = 1 - st
        QE, KE, KHT, KHTOK, VTOK, AM = TB[b]
        rB = [("tb", b, i) for i in range(6)]
        for n in range(8):
            pu = pU0 if n % 2 == 0 else pU1
            pun = pu[:, (n // 2) * 128:(n // 2 + 1) * 128]
            if n == 0 and ti4 == 0:
                cp("dve", SALL[st][:, 0, :], pun, r=["ps5"], w=[("sall", st)])
            else:
                prev = SALL[st][:, n - 1, :] if n > 0 else SALL[so][:, 7, :]
                rr = [("sall", st), "ps%d" % (5 + n % 2), ("ebl", b)] + ([("sall", so)] if n == 0 else [])
                stt(SALL[st][:, n, :], prev, EBL[b][:, n:n + 1], pun, ALU.mult, ALU.add, r=rr, w=[("sall", st)])
        if ti4 == 0:
            P.op("pool", lambda e, o=SBF[st][:, 0, :]: e.memset(o, 0.0), r=[], w=[("sbf", st)])
        else:
            cp("pool", SBF[st][:, 0, :], SALL[so][:, 7, :], r=[("sall", so)], w=[("sbf", st)])
        if ti4 == 3:
            cp("pool", GS3[:, hd, :], SALL[st][:, 7, :], r=[("sall", st)], w=["GS"])

    def Bcast(k):
        st = k % 2
        cp("act", SBF[st][:, 1:8, :], SALL[st][:, 0:7, :], r=[("sall", st)], w=[("sbf", st)])

    def Bo(k):
        hd, ti4, ti, t0, to = tinfo(k)
        b, st = k % 3, k % 2
        QE, KE, KHT, KHTOK, VTOK, AM = TB[b]
        rB = [("tb", b, i) for i in range(6)]
        for s in range(4):
            mm(po[:, s * 128:(s + 1) * 128], VTOK[:, s * 128:(s + 1) * 128], AM[:, s * 128:(s + 1) * 128], True, False,
               r=[rB[4], rB[5]], w=["ps7"])
            mm(po[:, s * 128:s * 128 + 64], SBF[st][:, 2 * s, :], QE[:, s * 128:s * 128 + 64], False, False,
               r=[("sbf", st), rB[0]], w=["ps7"])
            mm(po[:, s * 128 + 64:s * 128 + 128], SBF[st][:, 2 * s + 1, :], QE[:, s * 128 + 64:s * 128 + 128], False, True,
               r=[("sbf", st), rB[0]], w=["ps7"])
        cp("act", A3[:, hd, to:to + 512], po, r=["ps7"], w=[("A", hd, ti4)])

    load_wh(0)
    F1f(0)
    for step in range(NT + 2):
        if step < NT:
            F1(step)
            S1(step)
        if 0 <= step - 2 < NT:
            Bk(step - 2)
        if step < NT:
            E0(step)
        if 0 <= step - 2 < NT:
            Bcast(step - 2)
        if step < NT:
            E(step)
            Ec1(step)
        if 0 <= step - 2 < NT:
            Bo(step - 2)
        if 0 <= step - 1 < NT:
            F2(step - 1)
        if step < NT:
            Ec(step)
            Eb(step)
        if step + 1 < NT:
            F1f(step + 1)

    P.barrier()
    dma("sp", gsrc_d, GS, r=["GS"], w=["gsrc"], key="gs")
    if NO_CC:
        dma("sp", gdst_d[0:128, :], gsrc_d, r=["gsrc"], w=["gdst"], key="cc")
    else:
        P.op("pool", lambda e: e.collective_compute("AllGather", ALU.bypass, replica_groups=[[0, 1, 2, 3], [4, 5, 6, 7]],
                                                    ins=[gsrc_d], outs=[gdst_d]),
             r=["gsrc"], w=["gdst"], key="cc", unit=1)

    CW_OFF = X_OFF
    WSC = [bfv(CW_OFF + i * 1024, 1024).rearrange("p (c f) -> p c f", c=8) for i in range(2)]
    DG = [bfv(CW_OFF + 2048 + 64 * k, 64) for k in range(31)]
    UB = [bfv(CW_OFF + 4032 + i * (TT // 2), TT // 2) for i in range(2)]
    TC = [f32v(CW_OFF + 6112 + i * 512, 512) for i in range(2)]
    NDV = 3
    C_OFF3 = CW_OFF + 7136
    assert C_OFF3 + 8192 <= X_OFF + XW
    C3 = bfv(C_OFF3, 8192).rearrange("p (c t) -> p c t", c=8)
    def emit_combine():
        gd3 = gdst_d.rearrange("(r p) c -> p r c", p=128)
        for r_ in range(NR):
            dma("sp", BT3[:, r_, :], gd3[:, r_, 1024:1032], r=["gdst"], w=["BT"], key="bt")
        act(EB_[:, 0:8 * NR], BT[:, 0:8 * NR], AF.Exp, r=["BT"], w=["EB_"])
        P.op("dve", lambda e: e.memset(RS, 0.0), r=[], w=["RS"])
        for r_ in range(NR):
            gl = GL[r_ % 2]
            dma("sp", gl, gd3[:, r_, 0:1024], r=["gdst"], w=[("gl", r_ % 2)], key=("gl", r_ % 2))
            ts("dve", CR[:, r_ * 8:(r_ + 1) * 8], EB_[:, r_ * 8:(r_ + 1) * 8], INC[:, r_:r_ + 1], INC[:, 8 + r_:9 + r_],
               ALU.mult, ALU.add, r=["EB_", "INC"], w=[("cr", r_)])
            ts("dve", gl, gl, INC[:, r_:r_ + 1], None, ALU.mult, None, r=[("gl", r_ % 2), "INC"], w=[("gl", r_ % 2)])
            for h in range(8):
                stt(RS3[:, h, :], RS3[:, h, :], CR[:, r_ * 8 + h:r_ * 8 + h + 1], gl[:, h * 128:(h + 1) * 128], ALU.mult, ALU.add,
                    r=["RS", ("cr", r_), ("gl", r_ % 2)], w=["RS"])
        cp("act", SIN, RS3, r=["RS"], w=["SIN"])


    for cc in range(8):
        sw = cc % 2
        if cc == 5:
            emit_combine()
        dma("pool", WSC[sw], wcv_d[cc], r=[], w=[("wsc", sw)], key=("wsc", sw))
        for k in range(31 - NDV):
            ts("dve", DG[k], IDENT, CONVW[:, cc, k:k + 1], None, ALU.mult, None, r=["IDENT", "SM"], w=[("dg", k)])
        for k5, ((t0, w), ti) in enumerate(zip(TILES_ALL, ALL_IDS)):
            st = (cc * 5 + k5) % 2
            T1 = TC[st]
            pcv, pcg = PS[st], PS[2 + st]
            for c in range(8):
                mm(pcv[:, :w], WSC[sw][:, c, 0:128], H3[:, c, t0:t0 + w], c == 0, c == 7, r=[("wsc", sw), hr(c, ti)], w=["ps%d" % st])
            for c in range(8):
                mm(pcg[:, :w], WSC[sw][:, c, 128:256], H3[:, c, t0:t0 + w], c == 0, c == 7, r=[("wsc", sw), hr(c, ti)], w=["ps%d" % (2 + st)])
            act(T1[:, :w], pcg[:, :w], AF.Sigmoid, r=["ps%d" % (2 + st)], w=[("tc", st)])
            tt("dve", UB[sw][:, t0:t0 + w], pcv[:, :w], T1[:, :w], ALU.mult, r=["ps%d" % st, ("tc", st)], w=[("ub", sw, ti)])
        NPE = 31 - NDV
        for ti4 in range(4):
            t0 = HALO + 512 * ti4
            to = 512 * ti4
            pcn = PS[4 + ti4 % 4]
            rr = [("ub", sw, ti4 + 1), ("ub", sw, ti4)]
            acc = TC[ti4 % 2]
            for k in range(NPE, 31):
                src = UB[sw][:, t0 - 30 + k:t0 - 30 + k + 512]
                if k == NPE:
                    ts("dve", acc, src, CONVW[:, cc, k:k + 1], None, ALU.mult, None, r=rr[:2] + ["SM"], w=[("tc", ti4 % 2)])
                else:
                    stt(acc, src, CONVW[:, cc, k:k + 1], acc, ALU.mult, ALU.add, r=rr[:2] + ["SM", ("tc", ti4 % 2)], w=[("tc", ti4 % 2)])
            for k in range(NPE):
                mm(pcn, DG[k], UB[sw][:, t0 - 30 + k:t0 - 30 + k + 512], k == 0, k == NPE - 1, r=rr + [("dg", k)], w=["ps%d" % (4 + ti4 % 4)])
            stt(C3[:, cc, to:to + 512], pcn, CONVB[:, cc:cc + 1], acc, ALU.add, ALU.add,
                r=["ps%d" % (4 + ti4 % 4), "SM", ("tc", ti4 % 2)], w=[("C", cc, ti4)])
    P.barrier()

    ZW_OFF = CW_OFF
    TZ = [[f32v(ZW_OFF + (i * 3 + j) * 512, 512) for j in range(3)] for i in range(2)]
    SQZ = [bfv(ZW_OFF + 3072 + i * 256, 256) for i in range(2)]
    WHG = [bfv(ZW_OFF + 3584 + i * 512, 512).rearrange("p (c f) -> p c f", c=8) for i in range(2)]
    WSA = [bfv(ZW_OFF + 4608 + i * 512, 512).rearrange("p (c f) -> p c f", c=8) for i in range(2)]
    WSB = [bfv(ZW_OFF + 5632 + i * 512, 512).rearrange("p (c f) -> p c f", c=8) for i in range(2)]
    assert ZW_OFF + 6656 <= C_OFF3
    WSD = [bfv(C_OFF3 + 8192 + i * 512, 512).rearrange("p (c f) -> p c f", c=8) for i in range(2)]
    assert C_OFF3 + 8192 + 1024 <= X_OFF + XW

    def rstd_from(out, in_, scale, r, w):
        act(out, in_, AF.Ln, r=r, w=w, bias=EPSB, scale=scale)
        act(out, out, AF.Exp, r=w, w=w, scale=-0.5)

    T1Z = [TZ[0][0], TZ[1][0], f32v(R_OFF + 5856, 512)]
    assert R_OFF + 5856 + 512 <= NW
    T3Z = [TZ[0][2], TZ[1][2], f32v(X_OFF + XW - 512, 512)]
    assert C_OFF3 + 8192 + 1024 <= X_OFF + XW - 512 or True

    def zinfo(k):
        hd, ti4 = k // 4, k % 4
        return hd, ti4, ti4 + 1, HALO + 512 * ti4, 512 * ti4

    def Z1(k):
        hd, ti4, ti, t0, to = zinfo(k)
        s_w = hd % 2
        if ti4 == 0:
            dma("pool", WHG[s_w], whg_d[hd, :, :, 384:512], r=[], w=[("whg", s_w)], key=("whg", s_w))
        pc, pg = PS[k % 2], PS[4 + k % 3]
        mm(pc, SIN[:, hd, :], B3[:, hd, to:to + 512], True, True, r=["SIN", ("B", hd, ti4)], w=["ps%d" % (k % 2)])
        for c in range(8):
            mm(pg, WHG[s_w][:, c, :], H3[:, c, t0:t0 + 512], c == 0, c == 7, r=[("whg", s_w), hr(c, ti)], w=["ps%d" % (4 + k % 3)])
        tt("dve", T1Z[k % 3], pc, A3[:, hd, to:to + 512], ALU.add, r=["ps%d" % (k % 2), ("A", hd, ti4)], w=[("t1z", k % 3)])

    def Z2(k):
        st = k % 2
        T3 = T3Z[k % 3]
        pg = PS[4 + k % 3]
        act(SQZ[st], T1Z[k % 3], AF.Square, r=[("t1z", k % 3)], w=[("sqz", st)])
        sigmoid_act(T3, pg, r=["ps%d" % (4 + k % 3)], w=[("t3z", k % 3)])
        tt("dve", T3, pg, T3, ALU.mult, r=["ps%d" % (4 + k % 3), ("t3z", k % 3)], w=[("t3z", k % 3)])

    def Z3(k):
        hd, ti4, ti, t0, to = zinfo(k)
        st = k % 2
        T2, T3 = TZ[st][1], T3Z[k % 3]
        T1 = T1Z[k % 3]
        pn = PS[2 + st]
        mm(pn, ONES, SQZ[st], True, True, r=["ONES", ("sqz", st)], w=["ps%d" % (2 + st)])
        rstd_act(T2, pn, 1.0 / 128, r=["ps%d" % (2 + st)], w=[("tz", st, 1)])
        stt(T1, T1, HN, T2, ALU.mult, ALU.mult, r=[("t1z", k % 3), ("tz", st, 1), "SM"], w=[("t1z", k % 3)])
        tt("dve", A3[:, hd, to:to + 512], T1, T3, ALU.mult, r=[("t1z", k % 3), ("t3z", k % 3)], w=[("A", hd, ti4)])

    def Z23(k2, k3):
        a2 = k2 is not None
        a3 = k3 is not None
        if a2:
            st2 = k2 % 2
            T3 = T3Z[k2 % 3]
            pg = PS[4 + k2 % 3]
            rg = "ps%d" % (4 + k2 % 3)
            w3 = [("t3z", k2 % 3)]
        if a3:
            hd, ti4, ti, t0, to = zinfo(k3)
            st3 = k3 % 2
            T2, T3b, T1 = TZ[st3][1], T3Z[k3 % 3], T1Z[k3 % 3]
            pn = PS[2 + st3]
            rn = "ps%d" % (2 + st3)
            w2 = [("tz", st3, 1)]
            mm(pn, ONES, SQZ[st3], True, True, r=["ONES", ("sqz", st3)], w=[rn])
        if a2:
            act(SQZ[st2], T1Z[k2 % 3], AF.Square, r=[("t1z", k2 % 3)], w=[("sqz", st2)])
            act(T3, pg, AF.Exp, r=[rg], w=w3, scale=-1.0)
        if a3:
            act(T2, pn, AF.Ln, r=[rn], w=w2, bias=EPSB, scale=1.0 / 128)
        if a2:
            act(T3, T3, AF.Ln, r=w3, w=w3, bias=ONEB, scale=1.0)
        if a3:
            act(T2, T2, AF.Exp, r=w2, w=w2, scale=-0.5)
        if a2:
            act(T3, T3, AF.Exp, r=w3, w=w3, scale=-1.0)
            tt("dve", T3, pg, T3, ALU.mult, r=[rg] + w3, w=w3)
        if a3:
            stt(T1, T1, HN, T2, ALU.mult, ALU.mult, r=[("t1z", k3 % 3)] + w2 + ["SM"], w=[("t1z", k3 % 3)])
            tt("dve", A3[:, hd, to:to + 512], T1, T3b, ALU.mult, r=[("t1z", k3 % 3), ("t3z", k3 % 3)], w=[("A", hd, ti4)])

    for i in range(34):
        if i < 32:
            Z1(i)
        Z23(i - 1 if 0 <= i - 1 < 32 else None, i - 2 if 0 <= i - 2 < 32 else None)
    P.barrier()

    MEAN = f32v(R_OFF, T)
    LRS = f32v(R_OFF + T, T)
    for ti4 in range(4):
        to = 512 * ti4
        st = ti4 % 2
        pss, psq = PS[st], PS[2 + st]
        for cc in range(8):
            mm(pss, ONES, C3[:, cc, to:to + 512], cc == 0, cc == 7, r=["ONES", ("C", cc, ti4)], w=["ps%d" % st])
        for cc in range(8):
            sq = SQZ[cc % 2]
            act(sq, C3[:, cc, to:to + 512], AF.Square, r=[("C", cc, ti4)], w=[("sqz", cc % 2)])
            mm(psq, ONES, sq, cc == 0, cc == 7, r=["ONES", ("sqz", cc % 2)], w=["ps%d" % (2 + st)])
        T1 = TZ[st][0]
        act(MEAN[:, to:to + 512], pss, AF.Copy, r=["ps%d" % st], w=[("mean", ti4)], scale=1.0 / D)
        tt("dve", T1, MEAN[:, to:to + 512], MEAN[:, to:to + 512], ALU.mult, r=[("mean", ti4)], w=[("tz", st, 0)])
        stt(T1, psq, 1.0 / D, T1, ALU.mult, ALU.subtract, r=["ps%d" % (2 + st), ("tz", st, 0)], w=[("tz", st, 0)])
        act(LRS[:, to:to + 512], T1, AF.Ln, r=[("tz", st, 0)], w=[("lrs", ti4)], bias=EPSB, scale=1.0)
        act(LRS[:, to:to + 512], LRS[:, to:to + 512], AF.Exp, r=[("lrs", ti4)], w=[("lrs", ti4)], scale=-0.5)
    def ln_apply(ti4):
        to = 512 * ti4
        for cc in range(8):
            st = cc % 2
            T1 = TZ[st][2]
            tt("dve", T1, C3[:, cc, to:to + 512], MEAN[:, to:to + 512], ALU.subtract, r=[("C", cc, ti4), ("mean", ti4)], w=[("tz", st, 2)])
            tt("pool", T1, T1, LRS[:, to:to + 512], ALU.mult, r=[("tz", st, 2), ("lrs", ti4)], w=[("tz", st, 2)])
            act(C3[:, cc, to:to + 512], T1, AF.Silu, r=[("tz", st, 2), "SM"], w=[("C", cc, ti4)],
                bias=LNB[:, cc:cc + 1], scale=LNG[:, cc:cc + 1])

    def load_merged(o):
        sw = o % 2
        dma("pool", WSA[sw], wo_d[o], r=[], w=[("wsa", sw)], key=("wsa", sw))
        dma("pool", WHG[sw], wpw_d[o], r=[], w=[("whg", sw)], key=("whg", sw))
        dma("pool", WSB[sw], wgab_d[o][:, :, 0:128], r=[], w=[("wsb", sw)], key=("wsb", sw))
        dma("pool", WSD[sw], wgab_d[o][:, :, 128:256], r=[], w=[("wsd", sw)], key=("wsd", sw))

    load_merged(0)
    load_merged(1)
    WG2 = bfv(R_OFF, 2048).rearrange("p (c f) -> p c f", c=8)
    WU2 = bfv(R_OFF + 2048, 2048).rearrange("p (c f) -> p c f", c=8)
    lnres = [("mean", t_) for t_ in range(4)] + [("lrs", t_) for t_ in range(4)]
    WOVR[(1, 0)] = (WG2, WU2, "wg2", "wu2")
    for o in range(8):
        sw = o % 2
        if o == 1:
            dma("pool", WG2, wg_d[1][:, :, 0:512], r=[], w=["wg2"] + lnres, key="wg2")
            dma("pool", WU2, wu_d[1][:, :, 0:512], r=[], w=["wu2"] + lnres, key="wu2")
        if 1 <= o and o + 1 < 8:
            load_merged(o + 1)
        for ti4 in range(4):
            if o == 0:
                ln_apply(ti4)
            st = (o * 4 + ti4) % 2
            ti = ti4 + 1
            t0 = HALO + 512 * ti4
            to = 512 * ti4
            T1, T2, T3 = TZ[st]
            pya, pga, pyb, pgb = PS[st], PS[2 + st], PS[4 + st], PS[6 + st]
            for h in range(8):
                mm(pya, WSA[sw][:, h, :], A3[:, h, to:to + 512], h == 0, h == 7, r=[("wsa", sw), ("A", h, ti4)], w=["ps%d" % st])
            for c in range(8):
                mm(pga, WSB[sw][:, c, :], H3[:, c, t0:t0 + 512], c == 0, c == 7, r=[("wsb", sw), hr(c, ti)], w=["ps%d" % (2 + st)])
            for c in range(8):
                mm(pgb, WSD[sw][:, c, :], H3[:, c, t0:t0 + 512], c == 0, c == 7, r=[("wsd", sw), hr(c, ti)], w=["ps%d" % (6 + st)])
            for cc in range(8):
                mm(pyb, WHG[sw][:, cc, :], C3[:, cc, to:to + 512], cc == 0, cc == 7, r=[("whg", sw), ("C", cc, ti4)], w=["ps%d" % (4 + st)])
            act(T1, pga, AF.Sigmoid, r=["ps%d" % (2 + st)], w=[("tz", st, 0)])
            act(T2, pgb, AF.Sigmoid, r=["ps%d" % (6 + st)], w=[("tz", st, 1)])
            tt("dve", T1, pya, T1, ALU.mult, r=["ps%d" % st, ("tz", st, 0)], w=[("tz", st, 0)])
            stt(T2, pyb, PWB[:, o:o + 1], T2, ALU.add, ALU.mult, r=["ps%d" % (4 + st), ("tz", st, 1), "SM"], w=[("tz", st, 1)])
            tt("pool", B3[:, o, to:to + 512], T1, T2, ALU.add, r=[("tz", st, 0), ("tz", st, 1)], w=[("B", o, ti4)])
    P.barrier()

    for c in range(8):
        dma("sp", X3[:, c, :], xsp_d[:, c * TT:(c + 1) * TT], r=[("xsp", c)], w=[xr(c, ti) for ti in range(5)], key=("xrl", c))
    WS2 = [bfv(R_OFF + 4096 + i * 512, 512).rearrange("p (c f) -> p c f", c=8) for i in range(2)]
    for o in range(8):
        sw = o % 2
        dma("pool", WS2[sw], wout_d[o], r=[], w=[("ws2", sw)], key=("ws2", sw))
        for ti4 in range(4):
            st = (o * 4 + ti4) % 4
            ti = ti4 + 1
            t0 = HALO + 512 * ti4
            to = 512 * ti4
            px = PS[st]
            for m in range(8):
                mm(px, WS2[sw][:, m, :], B3[:, m, to:to + 512], m == 0, m == 7, r=[("ws2", sw), ("B", m, ti4)], w=["ps%d" % st])
            tt("dve", X3[:, o, t0:t0 + 512], px, X3[:, o, t0:t0 + 512], ALU.add, r=["ps%d" % st, xr(o, ti)], w=[xr(o, ti)])
    P.barrier()

    if DEBUG_STAGE == "mix":
        dump_X()
        P.op("sp", None, r=[("y", c) for c in range(8)], w=[])
        return nc, P, (arena_cm, ps_cms)

    ffn(1, TILES_OWN, OWN_IDS, G3)
    P.barrier()
    if DEBUG_STAGE == "ffn2":
        dump_X()
        P.op("sp", None, r=[("y", c) for c in range(8)], w=[])
        return nc, P, (arena_cm, ps_cms)
    for k, ((t0, w), ti) in enumerate(zip(TILES_OWN, OWN_IDS)):
        pn = PS[7]
        for c in range(8):
            sq = FM["sq"][c % 4]
            act(sq, X3[:, c, t0:t0 + w], AF.Square, r=[xr(c, ti)], w=[("sq", c % 4)])
            mm(pn, ONES, sq, c == 0, c == 7, r=["ONES", ("sq", c % 4)], w=["ps7"])
        rstd_act(RSTD[:, t0:t0 + w], pn, 1.0 / D, r=["ps7"], w=[("rstd", ti)])
        ost = FM["ost"][k % 2]
        for c in range(8):
            stt(ost[:, c, :], X3[:, c, t0:t0 + w], GF[:, c:c + 1], RSTD[:, t0:t0 + w], ALU.mult, ALU.mult,
                r=[xr(c, ti), ("rstd", ti), "SM"], w=[("ost", k % 2)])
        dma("sp", y_d[:, :, (t0 - HALO):(t0 - HALO) + 512], ost, r=[("ost", k % 2)], w=[("y", k)], key="out")
    P.op("sp", None, r=[("y", k) for k in range(4)], w=[])
    return nc, P, (arena_cm, ps_cms)


def emit_program(nc, P):
    keys = []
    for o in P.ops:
        if o.key is not None and o.key not in keys:
            keys.append(o.key)
    sem_cms = []

    def newsem(name):
        cm = nc.semaphore(name)
        sem_cms.append(cm)
        return cm.__enter__()

    esems = {e: newsem("se_" + e) for e in Prog.ENGS}
    dsems = {k: newsem("sd_%d" % i) for i, k in enumerate(keys)}
    streams = P.emit(nc, None, esems, dsems)
    with nc.Block() as block:
        @block.tensor
        def _(e):
            run_stream(streams["pe"], e, esems, dsems)

        @block.scalar
        def _(e):
            run_stream(streams["act"], e, esems, dsems)

        @block.vector
        def _(e):
            run_stream(streams["dve"], e, esems, dsems)

        @block.gpsimd
        def _(e):
            run_stream(streams["pool"], e, esems, dsems)

        @block.sync
        def _(e):
            run_stream(streams["sp"], e, esems, dsems)
    return nc


def _pm(w, nchunk):
    return np.ascontiguousarray(w.reshape(nchunk, 128, -1).transpose(1, 0, 2))


def _colblocks(w_pm, starts, width):
    return np.ascontiguousarray(np.concatenate([w_pm[:, :, s:s + width] for s in starts], axis=2))


def kernel(x, ffn1_norm, ffn1_w_gate, ffn1_w_up, ffn1_w_down, mix_norm, w_in, hgrn_lb_logits, hgrn_head_norm,
           hgrn_w_o, conv_w, conv_b, conv_ln_g, conv_ln_b, conv_w_pw, conv_b_pw, w_out, ffn2_norm, ffn2_w_gate,
           ffn2_w_up, ffn2_w_down, final_norm):
    f = lambda a: np.asarray(a, dtype=np.float32)
    x = f(x)
    win = _pm(f(w_in)[0], 8)
    shared = {
        "wg1": _pm(f(ffn1_w_gate)[0], 8), "wu1": _pm(f(ffn1_w_up)[0], 8), "wd1": _pm(f(ffn1_w_down)[0], NFC),
        "wg2": _pm(f(ffn2_w_gate)[0], 8), "wu2": _pm(f(ffn2_w_up)[0], 8), "wd2": _pm(f(ffn2_w_down)[0], NFC),
        "whg": np.stack([_colblocks(win, [h * 128, 1024 + h * 128, 2048 + h * 128, 3072 + h * 128], 128) for h in range(8)]),
        "wcv": np.stack([_colblocks(win, [4096 + c * 128, 5120 + c * 128], 128) for c in range(8)]),
        "wgab": np.stack([_colblocks(win, [6144 + c * 128, 7168 + c * 128], 128) for c in range(8)]),
    }

    def oblocks(w):
        wp = _pm(f(w)[0], 8)
        return np.stack([np.ascontiguousarray(wp[:, :, o * 128:(o + 1) * 128]) for o in range(8)])

    shared["wo"] = oblocks(hgrn_w_o)
    shared["wpw"] = oblocks(conv_w_pw)
    shared["wout"] = oblocks(w_out)

    def col8(v):
        return f(v).reshape(8, 128).T

    NSM = 8 * 13 + 2 + 8 * 31 + 2 + 24
    small = np.zeros((128, NSM), np.float32)
    small[:, 0:8] = col8(ffn1_norm[0])
    small[:, 8:16] = col8(mix_norm[0])
    small[:, 16:24] = col8(ffn2_norm[0])
    small[:, 24:32] = col8(final_norm)
    lbl = f(hgrn_lb_logits)
    small[:, 32:40] = col8(lbl[0])
    small[:, 40:48] = col8(lbl[1])
    small[:, 48:56] = col8(conv_b[0])
    small[:, 56:64] = col8(conv_ln_g[0])
    small[:, 64:72] = col8(conv_ln_b[0])
    small[:, 72:80] = col8(conv_b_pw[0])
    small[:, 96] = f(hgrn_head_norm)[0]
    small[:, 97] = EPS
    small[:, 354] = 1.0
    cw = f(conv_w)[0]
    small[:, 106:106 + 248] = cw.T.reshape(8, 128, 31).transpose(1, 0, 2).reshape(128, 248)
    shared["small"] = small

    cst = np.zeros((128, 128 * 2 + 512 * 3), np.float32)
    cst[:, 0:128] = np.eye(128, dtype=np.float32)
    cst[:, 128:256] = 1.0
    s_idx = np.arange(128)[:, None]
    t_idx = np.arange(128)[None, :]
    cm = ((s_idx // 64 == t_idx // 64) & (s_idx <= t_idx)).astype(np.float32)
    cst[:, 256:768] = np.tile(cm, (1, 4))
    rs = np.ones(512, np.float32)
    rs[0::64] = 0.0
    cst[:, 768:1280] = rs[None, :]
    cst[:, 1280:1792] = 1.0
    shared["cst"] = cst

    in_maps = []
    for c in range(NCORES):
        b, j = c // 4, c % 4
        own = x[b, j * T:(j + 1) * T, :]
        halo = x[b, j * T - HALO:j * T, :] if j > 0 else np.zeros((HALO, D), np.float32)
        xc = np.concatenate([halo, own], axis=0).T
        xc = xc.reshape(8, 128, TT).transpose(1, 0, 2).reshape(128, 8 * TT)
        inc = np.zeros((128, 16), np.float32)
        for r_ in range(4):
            v = 1.0 if r_ < j else 0.0
            inc[:, r_] = v
            inc[:, 8 + r_] = 1.0 - v
        m = dict(shared)
        m["xT"] = np.ascontiguousarray(xc)
        m["inc"] = inc
        in_maps.append(m)

    nc, P, keep = build_program()
    emit_program(nc, P)
    res = run_bass_kernel_spmd(nc, in_maps, core_ids=list(range(NCORES)))
    out = np.empty((2, 4 * T, D), np.float32)
    for c in range(NCORES):
        b, j = c // 4, c % 4
        yc = np.asarray(res.results[c]["y"]).reshape(128, 8, T)
        out[b, j * T:(j + 1) * T, :] = yc.transpose(2, 1, 0).reshape(T, D)
    return out
```
